# Optimizing a Trainium2 kernel written in Bass

```python
import math
import jax, jax.numpy as jnp
from jax import lax
import numpy as np

D_MODEL = 1024
BATCH = 8
SEQ = 4096
DEPTH = 4

N_META = 16
EXPAND = 2
D_INNER = EXPAND * D_MODEL
GLA_HEADS = 4
GLA_DK = D_INNER // 2
GLA_DV = D_INNER
GLA_HK = GLA_DK // GLA_HEADS
GLA_HV = GLA_DV // GLA_HEADS
GLA_RANK = 16
GLA_TAU = 16.0
GLA_CHUNK = 64
GLA_PROJ = 2 * GLA_DK + GLA_DV + D_INNER + GLA_RANK
DIFF_HEAD_DIM = 128
DIFF_HEADS = D_INNER // (2 * DIFF_HEAD_DIM)
DIFF_PROJ = 4 * D_INNER
DIFF_QBLOCK = 128
REL_BUCKETS = 32
REL_MAX_EXACT = 16
REL_MAX_DIST = 128
EPS = 1e-6
N_GLA = (DEPTH + 1) // 2
N_DIFF = DEPTH // 2

kernel_name = "hybrid_gla_diffattn_meta_trunk"


def rmsnorm(x, g):
    xf = x.astype(jnp.float32)
    y = xf * lax.rsqrt(jnp.mean(xf * xf, axis=-1, keepdims=True) + EPS)
    return (y * g.astype(jnp.float32)).astype(x.dtype)


def head_rms(o):
    return o * lax.rsqrt(jnp.mean(o * o, axis=-1, keepdims=True) + EPS)


def gla_chunk(S, inp):
    q, k, v, lg = inp
    c = q.shape[2]
    b = jnp.cumsum(lg, axis=2)
    causal = jnp.tril(jnp.ones((c, c), dtype=bool))
    rel = b[:, :, :, None, :] - b[:, :, None, :, :]
    decay = jnp.exp(jnp.where(causal[:, :, None], rel, -jnp.inf))
    attn = jnp.einsum('bhid,bhjd,bhijd->bhij', q, k, decay)
    o = (jnp.einsum('bhij,bhjv->bhiv', attn, v)
         + jnp.einsum('bhid,bhdv->bhiv', q * jnp.exp(b), S))
    b_last = b[:, :, -1:, :]
    S_new = (jnp.exp(b_last[:, :, 0, :])[..., None] * S
             + jnp.einsum('bhjd,bhjv->bhdv', k * jnp.exp(b_last - b), v))
    return S_new, o


def gla_branch(xn, w_in, w_gate_up, b_gate, gnorm, w_out):
    B, L, _ = xn.shape
    f32 = jnp.float32
    proj = xn @ w_in
    q, k, v, z, g_lr = jnp.split(
        proj, [GLA_DK, 2 * GLA_DK, 2 * GLA_DK + GLA_DV, 2 * GLA_DK + GLA_DV + D_INNER], axis=-1)
    lg = jax.nn.log_sigmoid((g_lr @ w_gate_up + b_gate).astype(f32)) / GLA_TAU

    def heads(t, hd):
        return t.astype(f32).reshape(B, L, GLA_HEADS, hd).transpose(0, 2, 1, 3)

    q = heads(q, GLA_HK) * (GLA_HK ** -0.5)
    k, lg, v = heads(k, GLA_HK), heads(lg, GLA_HK), heads(v, GLA_HV)

    S0 = jnp.zeros((B, GLA_HEADS, GLA_HK, GLA_HV), f32)
    S1, o_meta = gla_chunk(S0, (q[:, :, :N_META], k[:, :, :N_META],
                                v[:, :, :N_META], lg[:, :, :N_META]))
    n_chunks = (L - N_META) // GLA_CHUNK

    def chunks(t):
        t = t[:, :, N_META:]
        return jnp.moveaxis(t.reshape(B, GLA_HEADS, n_chunks, GLA_CHUNK, t.shape[-1]), 2, 0)

    _, o_real = lax.scan(gla_chunk, S1, (chunks(q), chunks(k), chunks(v), chunks(lg)))
    o_real = jnp.moveaxis(o_real, 0, 2).reshape(B, GLA_HEADS, L - N_META, GLA_HV)
    o = jnp.concatenate([o_meta, o_real], axis=2)
    o = head_rms(o.transpose(0, 2, 1, 3)).reshape(B, L, D_INNER) * gnorm.astype(f32)
    y = jax.nn.silu(z.astype(f32)) * o
    return (y.astype(xn.dtype) @ w_out)


def t5_bucket(n):
    nf = jnp.maximum(n, 1).astype(jnp.float32)
    large = REL_MAX_EXACT + (jnp.log(nf / REL_MAX_EXACT)
                             / math.log(REL_MAX_DIST / REL_MAX_EXACT)
                             * (REL_BUCKETS - REL_MAX_EXACT)).astype(jnp.int32)
    large = jnp.minimum(large, REL_BUCKETS - 1)
    return jnp.where(n < REL_MAX_EXACT, n, large)


def diff_branch(xn, w_in, lam_vecs, gnorm, w_out, rel_table, lambda_init):
    B, L, _ = xn.shape
    H, d = DIFF_HEADS, DIFF_HEAD_DIM
    f32 = jnp.float32
    proj = xn @ w_in
    q, k, v, z = jnp.split(proj, 4, axis=-1)
    lv = lam_vecs.astype(f32)
    lam = jnp.exp(jnp.sum(lv[0] * lv[1])) - jnp.exp(jnp.sum(lv[2] * lv[3])) + lambda_init

    n_blocks = -(-L // DIFF_QBLOCK)
    Lp = n_blocks * DIFF_QBLOCK
    pad = ((0, 0), (0, Lp - L), (0, 0))
    q = jnp.pad(q.astype(f32), pad).reshape(B, n_blocks, DIFF_QBLOCK, H, 2, d)
    q_blocks = q.transpose(1, 0, 3, 4, 2, 5)
    kh = jnp.pad(k.astype(f32), pad).reshape(B, Lp, H, 2, d).transpose(0, 2, 3, 1, 4)
    vh = jnp.pad(v.astype(f32), pad).reshape(B, Lp, H, 2 * d).transpose(0, 2, 1, 3)
    table = rel_table.astype(f32)
    kpos = jnp.arange(Lp, dtype=jnp.int32)
    scale = d ** -0.5

    def block(args):
        qb, start = args
        qpos = start + jnp.arange(DIFF_QBLOCK, dtype=jnp.int32)
        dist = qpos[:, None] - kpos[None, :]
        bias = jnp.moveaxis(table[t5_bucket(jnp.maximum(dist, 0))], -1, 0)
        s = jnp.einsum('bhpqd,bhpkd->bhpqk', qb, kh) * scale + bias[None, :, None]
        s = jnp.where(dist >= 0, s, -jnp.inf)
        p = jax.nn.softmax(s, axis=-1)
        a = p[:, :, 0] - lam * p[:, :, 1]
        return jnp.einsum('bhqk,bhkv->bhqv', a, vh)

    starts = jnp.arange(n_blocks, dtype=jnp.int32) * DIFF_QBLOCK
    o = lax.map(block, (q_blocks, starts))
    o = o.transpose(1, 0, 3, 2, 4).reshape(B, Lp, H, 2 * d)[:, :L]
    o = head_rms(o) * gnorm.astype(f32) * (1.0 - lambda_init)
    y = jax.nn.silu(z.astype(f32)) * o.reshape(B, L, D_INNER)
    return (y.astype(xn.dtype) @ w_out)


def setup_inputs(seed: int = 0) -> dict:
    key = jax.random.key(seed)
    ks = jax.random.split(key, 16)
    nrm = jax.random.normal
    f32 = jnp.float32
    return {
        "x": nrm(ks[0], (BATCH, SEQ, D_MODEL), f32),
        "meta": nrm(ks[1], (N_META, D_MODEL), f32),
        "g_norm": 1.0 + 0.02 * nrm(ks[2], (DEPTH, D_MODEL), f32),
        "gla_w_in": nrm(ks[3], (N_GLA, D_MODEL, GLA_PROJ), f32) * D_MODEL ** -0.5,
        "gla_w_gate_up": nrm(ks[4], (N_GLA, GLA_RANK, GLA_DK), f32) * GLA_RANK ** -0.5,
        "gla_b_gate": 0.1 * nrm(ks[5], (N_GLA, GLA_DK), f32),
        "gla_gnorm": 1.0 + 0.02 * nrm(ks[6], (N_GLA, D_INNER), f32),
        "gla_w_out": nrm(ks[7], (N_GLA, D_INNER, D_MODEL), f32) * D_INNER ** -0.5,
        "diff_w_in": nrm(ks[8], (N_DIFF, D_MODEL, DIFF_PROJ), f32) * D_MODEL ** -0.5,
        "diff_lam": 0.1 * nrm(ks[9], (N_DIFF, 4, DIFF_HEAD_DIM), f32),
        "diff_gnorm": 1.0 + 0.02 * nrm(ks[10], (N_DIFF, 2 * DIFF_HEAD_DIM), f32),
        "diff_w_out": nrm(ks[11], (N_DIFF, D_INNER, D_MODEL), f32) * D_INNER ** -0.5,
        "rel_bias": 0.5 * nrm(ks[12], (REL_BUCKETS, DIFF_HEADS), f32),
        "g_final": 1.0 + 0.02 * nrm(ks[13], (D_MODEL,), f32),
    }


def reference(x, meta, g_norm, gla_w_in, gla_w_gate_up, gla_b_gate, gla_gnorm, gla_w_out,
              diff_w_in, diff_lam, diff_gnorm, diff_w_out, rel_bias, g_final):
    B = x.shape[0]
    h = jnp.concatenate([jnp.broadcast_to(meta.astype(x.dtype)[None], (B, N_META, D_MODEL)), x], axis=1)
    gi = 0
    di = 0
    for i in range(DEPTH):
        hn = rmsnorm(h, g_norm[i])
        if i % 2 == 0:
            y = gla_branch(hn, gla_w_in[gi], gla_w_gate_up[gi], gla_b_gate[gi],
                           gla_gnorm[gi], gla_w_out[gi])
            gi += 1
        else:
            lambda_init = 0.8 - 0.6 * math.exp(-0.3 * i)
            y = diff_branch(hn, diff_w_in[di], diff_lam[di], diff_gnorm[di], diff_w_out[di],
                            rel_bias, lambda_init)
            di += 1
        h = h + y.astype(h.dtype)
    return rmsnorm(h[:, N_META:], g_final)
```

```python
import math
from contextlib import ExitStack

import numpy as np
import concourse.bass as bass
import concourse.mybir as mybir
from concourse.bass_utils import run_bass_kernel_spmd

F32 = mybir.dt.float32
BF16 = mybir.dt.bfloat16
AF = mybir.ActivationFunctionType
ALU = mybir.AluOpType

D_MODEL = 1024
SEQ = 4096
N_META = 16
LTOK = SEQ + N_META
NT = 33
LP = NT * 128
D_INNER = 2048
GLA_PROJ = 6160
DIFF_PROJ = 8192
EPS = 1e-6
NEG = -30000.0


class Buf:
    __slots__ = ("name", "w", "r", "ap")

    def __init__(self, name, ap=None):
        self.name = name
        self.w = None
        self.r = []
        self.ap = ap


class Eng:
    def __init__(self, name, h, same_sync):
        self.name = name
        self.h = h
        self.sem = None
        self.semkey = None
        self.cnt = 0
        self.seen = {}
        self.same_sync = same_sync
        self.pending = False


def _compress(lst):
    m = {}
    for k, v in lst:
        if m.get(k, 0) < v:
            m[k] = v
    return list(m.items())


class FW:
    def __init__(self, nc, n_dma_sems=12):
        self.nc = nc
        self.sems = {}
        self.E = {
            "pe": Eng("pe", nc.tensor, False),
            "act": Eng("act", nc.scalar, True),
            "dve": Eng("dve", nc.vector, True),
            "pool": Eng("pool", nc.gpsimd, True),
            "sp": Eng("sp", nc.sync, True),
        }
        self._stack = None
        self.epoch = 0
        self.n_dma_sems = n_dma_sems
        self.dma_pool = {}
        self.dma_rr = {}
        self.bufs = []
        self.n_inst = 0
        self.n_wait = 0

    def _new_sem(self, key):
        h = self._stack.enter_context(self.nc.semaphore(key))
        self.sems[key] = h
        return h

    def start(self, stack):
        self._stack = stack
        for e in self.E.values():
            e.semkey = f"s_{e.name}_{self.epoch}"
            e.sem = self._new_sem(e.semkey)
        for q in ("sp", "pool"):
            self.dma_pool[q] = []
            for i in range(self.n_dma_sems):
                k = f"d_{q}_{i}"
                self._new_sem(k)
                self.dma_pool[q].append([k, 0])
            self.dma_rr[q] = 0

    def buf(self, name, ap=None):
        b = Buf(name, ap)
        self.bufs.append(b)
        return b

    def _wait(self, eng, key, val):
        if eng.seen.get(key, 0) >= val:
            return
        if key == eng.semkey:
            if not eng.same_sync:
                return
            assert val <= eng.cnt, f"{eng.name} self-wait on pending value"
        else:
            for o in self.E.values():
                if o.semkey == key:
                    assert val <= o.cnt, f"{eng.name} waits on pending {o.name} {val}>{o.cnt}"
        eng.h.wait_ge(self.sems[key], val)
        eng.seen[key] = val
        self.n_wait += 1

    @staticmethod
    def _deps(reads, writes):
        deps = {}

        def add(d):
            if d is None:
                return
            k, v = d
            if deps.get(k, 0) < v:
                deps[k] = v
        for b in reads:
            add(b.w)
        for b in writes:
            add(b.w)
            for d in b.r:
                add(d)
        return deps

    def op(self, en, fn, reads=(), writes=(), signal=True):
        eng = self.E[en]
        for k, v in self._deps(reads, writes).items():
            self._wait(eng, k, v)
        inst = fn(eng.h)
        self.n_inst += 1
        if signal:
            inst.then_inc(eng.sem, 1)
            eng.cnt += 1
            idx = eng.cnt
            eng.pending = False
        else:
            idx = eng.cnt + 1
            eng.pending = True
        d = (eng.semkey, idx)
        for b in reads:
            b.r.append(d)
            if len(b.r) > 48:
                b.r = _compress(b.r)
        for b in writes:
            b.w = d
            b.r = []
        return inst

    def dma(self, q, out, in_, reads=(), writes=(), **kw):
        eng = self.E[q]
        for k, v in self._deps(reads, writes).items():
            self._wait(eng, k, v)
        pool = self.dma_pool[q]
        i = self.dma_rr[q]
        self.dma_rr[q] = (i + 1) % len(pool)
        slot = pool[i]
        if slot[1] > 0:
            self._wait(eng, slot[0], slot[1])
        inst = eng.h.dma_start(out=out, in_=in_, **kw)
        slot[1] += 16
        inst.then_inc(self.sems[slot[0]], 16)
        self.n_inst += 1
        d = (slot[0], slot[1])
        for b in reads:
            b.r.append(d)
            if len(b.r) > 48:
                b.r = _compress(b.r)
        for b in writes:
            b.w = d
            b.r = []
        return inst

    def barrier(self):
        for e in self.E.values():
            assert not e.pending, f"{e.name} pending"
        targets = [(e.semkey, e.cnt) for e in self.E.values() if e.cnt > 0]
        dma_t = [(k, c) for pool in self.dma_pool.values() for k, c in pool if c > 0]
        for e in self.E.values():
            for k, v in targets:
                if k != e.semkey:
                    self._wait(e, k, v)
            for k, v in dma_t:
                self._wait(e, k, v)
        for b in self.bufs:
            b.w = None
            b.r = []

    def finish(self):
        sp = self.E["sp"]
        for e in self.E.values():
            assert not e.pending
            if e is not sp and e.cnt > 0:
                self._wait(sp, e.semkey, e.cnt)
        for pool in self.dma_pool.values():
            for k, c in pool:
                if c > 0:
                    self._wait(sp, k, c)


def slot_rows(order, t):
    if order == "pos":
        r0 = 128 * t
        return r0, min(128, LTOK - r0)
    if t < 32:
        return N_META + 128 * t, 128
    return 0, N_META


def build_program(layers=(0, 1, 2, 3), final=True, debug_h=False):
    nc = bass.Bass("TRN2", target_bir_lowering=False)
    din = lambda n, s: nc.dram_tensor(n, list(s), F32, kind="ExternalInput").ap()
    x_d = din("x", (SEQ, D_MODEL))
    meta_d = din("meta", (N_META, D_MODEL))
    gnorm_d = din("g_norm", (4, D_MODEL))
    gfin_d = din("g_final", (1, D_MODEL))
    gwin_d = din("gla_w_in", (2, 4, D_MODEL, 1552))
    gwg_d = din("gla_wg", (2, 17, 1024))
    ggn_d = din("gla_gnorm", (2, D_INNER))
    gwo_d = din("gla_w_out", (2, D_INNER, D_MODEL))
    dwin_d = din("diff_w_in", (2, 8, D_MODEL, 1024))
    dlam_d = din("diff_lam", (2, 512))
    dgn_d = din("diff_gnorm", (2, 256))
    dwo_d = din("diff_w_out", (2, D_INNER, D_MODEL))
    relE_d = din("relE", (8, 128, 3, 256))
    relb_d = din("rel_bias", (32, 8))
    cst_d = din("consts", (128, 5, 128))
    out_d = nc.dram_tensor("out", [SEQ, D_MODEL], F32, kind="ExternalOutput").ap()
    H_d = nc.dram_tensor("Hres", [LTOK, D_MODEL], F32, kind="ExternalOutput" if debug_h else "Internal").ap()
    Y_d = nc.dram_tensor("Yscr", [LP, D_INNER], BF16, kind="Internal").ap()

    st = ExitStack()
    with st:
        fw = FW(nc)
        fw.start(st)
        _n = [0]

        cur = [st]
        base_mark = {}

        def sb(shape, dt, name=None):
            _n[0] += 1
            nm = f"{name or 't'}_{_n[0]}"
            h = cur[0].enter_context(nc.sbuf_tensor(nm, list(shape), dt))
            return fw.buf(nm, h.ap() if hasattr(h, "ap") else h[:])

        ps_all = nc.alloc_psum_tensor("ps", [128, 8, 512], F32).ap()
        banks = [fw.buf(f"bank{i}", ps_all[:, i, :]) for i in range(8)]
        rr = [0]

        def bank(lo=0, hi=8):
            while True:
                i = rr[0]
                rr[0] = (rr[0] + 1) % 8
                if lo <= i < hi:
                    return banks[i]

        def bf(b):
            return b.ap.bitcast(BF16)

        Hb = fw.buf("H_dram")
        Yb = fw.buf("Y_dram")

        cst = sb([128, 5, 128], F32, "cst")
        fw.dma("sp", cst.ap, cst_d, writes=[cst])
        ident = sb([128, 128], BF16, "ident")
        fw.dma("pool", ident.ap, cst_d[:, 0, :], writes=[ident])
        U, LM, UM, LMM = (cst.ap[:, i, :] for i in (1, 2, 3, 4))

        fw.dma("pool", H_d[0:N_META, :], meta_d, writes=[Hb])
        for c in range(8):
            fw.dma("pool", H_d[N_META + 512 * c:N_META + 512 * (c + 1), :], x_d[512 * c:512 * (c + 1), :], writes=[Hb])

        xnT = sb([128, 8, LP], BF16, "xnT")
        junk = sb([128, D_MODEL], BF16, "junk")
        eps_t = sb([128, 1], F32, "eps")
        fw.op("dve", lambda e: e.memset(eps_t.ap, EPS), writes=[eps_t])

        cpow = sb([128, 2], F32, "cpow")
        fw.op("pool", lambda e: e.memset(cpow.ap[:, 0:1], -0.5), writes=[cpow])
        fw.op("pool", lambda e: e.memset(cpow.ap[:, 1:2], -1.0), writes=[cpow])

        def rstd_from_ss(ss_ap, tmp_ap, out_ap, n, bufs):
            fw.op("pool", lambda e: e.tensor_scalar(tmp_ap, ss_ap, 1.0 / n, EPS, ALU.mult, ALU.add),
                  reads=bufs, writes=bufs)
            fw.op("pool", lambda e: e.tensor_tensor(out_ap, tmp_ap, cpow.ap[:, 0:1], ALU.pow),
                  reads=bufs + [cpow], writes=bufs)

        def phase_A(li, order):
            hs2 = [sb([128, D_MODEL], F32, "hs") for _ in range(4)]
            stat = [sb([128, 4], F32, "stat") for _ in range(4)]
            xn2 = [sb([128, D_MODEL], BF16, "xn") for _ in range(4)]
            gA = sb([128, D_MODEL], F32, "gA")
            fw.dma("sp", gA.ap, gnorm_d[li:li + 1, :].partition_broadcast(128), writes=[gA])
            for t in range(NT):
                r0, nv = slot_rows(order, t)
                hs = hs2[t % 4]
                stt = stat[t % 4]
                xn = xn2[t % 4]
                if nv < 128:
                    fw.op("pool", lambda e: e.memset(hs.ap, 0.0), writes=[hs])
                if li == 0 and order == "real":
                    src = x_d[128 * t:128 * (t + 1), :] if t < 32 else meta_d
                    fw.dma("sp", hs.ap[0:nv, :], src, writes=[hs])
                else:
                    fw.dma("sp", hs.ap[0:nv, :], H_d[r0:r0 + nv, :], reads=[Hb], writes=[hs])
                fw.op("act", lambda e: e.activation(junk.ap, hs.ap, AF.Square, accum_out=stt.ap[:, 0:1]),
                      reads=[hs], writes=[junk, stt])
                rstd_from_ss(stt.ap[:, 0:1], stt.ap[:, 1:2], stt.ap[:, 2:3], D_MODEL, [stt])
                fw.op("dve", lambda e: e.scalar_tensor_tensor(xn.ap, hs.ap, stt.ap[:, 2:3], gA.ap,
                                                              ALU.mult, ALU.mult),
                      reads=[hs, stt, gA], writes=[xn])
                pb = bank()
                for kc in range(8):
                    fw.op("pe", lambda e, kc=kc: e.transpose(bf(pb)[:, kc * 128:(kc + 1) * 128],
                                                             xn.ap[:, kc * 128:(kc + 1) * 128], ident.ap),
                          reads=[xn, ident], writes=[pb], signal=(kc == 7))
                fw.op("dve", lambda e: e.tensor_copy(xnT.ap[:, :, t * 128:(t + 1) * 128],
                                                     bf(pb).rearrange("p (k n) -> p k n", k=8)),
                      reads=[pb], writes=[xnT])

        def next_cols(order, t):
            r0, nv = slot_rows(order, t)
            runs = []
            if order == "real":
                runs.append((0, nv, r0))
            else:
                lo = max(r0, N_META)
                if r0 < N_META:
                    runs.append((0, N_META - r0, SEQ + r0))
                if r0 + nv > lo:
                    runs.append((lo - r0, nv, lo - N_META))
            return runs

        def phase_D(li, order, w_out_d, is_last, fuse_next, wo_prefetched=False, next_diff=None):
            NB_ = 3
            if wo_prefetched:
                assert nc.sbuf_bytes_remaining == base_mark["diff"], "w_out prefetch aliasing broken"
            Wn = None
            if next_diff is not None:
                base_mark["wnext"] = nc.sbuf_bytes_remaining
                Wn = [sb([128, 8, 1024], BF16, "Wn") for _ in range(2)]
            wo = sb([128, 16, D_MODEL], BF16, "wo")
            wst = [sb([128, D_MODEL], F32, "wst") for _ in range(4)]
            if not wo_prefetched:
                for kc in range(8, 16, 4):
                    fw.dma("pool", wo.ap[:, kc:kc + 4, :],
                           w_out_d[kc * 128:(kc + 4) * 128, :].rearrange("(kc p) n -> p kc n", p=128), writes=[wo])
            for kc in range(0 if not wo_prefetched else 8, 8):
                stg = wst[kc % 4]
                fw.dma("sp", stg.ap, w_out_d[kc * 128:(kc + 1) * 128, :], writes=[stg])
                ce = ("act", "dve", "pool")[kc % 3]
                if ce == "act":
                    fw.op("act", lambda e: e.activation(wo.ap[:, kc, :], stg.ap, AF.Copy), reads=[stg], writes=[wo])
                else:
                    fw.op(ce, lambda e: e.tensor_copy(wo.ap[:, kc, :], stg.ap), reads=[stg], writes=[wo])
            if Wn is not None:
                for hh in range(2):
                    for kc4 in range(0, 8, 4):
                        fw.dma("pool", Wn[hh].ap[:, kc4:kc4 + 4, :],
                               dwin_d[next_diff][hh][kc4 * 128:(kc4 + 4) * 128, :].rearrange("(kc p) n -> p kc n", p=128),
                               writes=[Wn[hh]])
            ysb2 = [sb([128, D_INNER], BF16, "ysb") for _ in range(NB_)]
            yT2 = [sb([128, 16, 128], BF16, "yT") for _ in range(2)]
            hn2 = [sb([128, D_MODEL], F32, "hn") for _ in range(3)]
            ob2 = [sb([128, D_MODEL], F32, "ob") for _ in range(2)] if is_last else None
            hs2 = [sb([128, D_MODEL], F32, "hs") for _ in range(NB_)]
            stat = [sb([128, 4], F32, "stat") for _ in range(3)]
            gF = sb([128, D_MODEL], F32, "gF")
            xn2 = [sb([128, D_MODEL], BF16, "xnD") for _ in range(2)] if fuse_next else None
            if is_last:
                fw.dma("sp", gF.ap, gfin_d.partition_broadcast(128), writes=[gF])
            elif fuse_next:
                fw.dma("sp", gF.ap, gnorm_d[li + 1:li + 2, :].partition_broadcast(128), writes=[gF])

            def loads(t):
                r0, nv = slot_rows(order, t)
                fw.dma("sp", ysb2[t % NB_].ap, Y_d[t * 128:(t + 1) * 128, :], reads=[Yb], writes=[ysb2[t % NB_]])
                fw.dma("sp", hs2[t % NB_].ap[0:nv, :], H_d[r0:r0 + nv, :], reads=[Hb], writes=[hs2[t % NB_]])

            def stage_T(t):
                ysb, yT = ysb2[t % NB_], yT2[t % 2]
                for half in range(2):
                    pb = bank()
                    for k in range(8):
                        kc = half * 8 + k
                        fw.op("pe", lambda e, k=k, kc=kc: e.transpose(bf(pb)[:, k * 128:(k + 1) * 128],
                                                                      ysb.ap[:, kc * 128:(kc + 1) * 128], ident.ap),
                              reads=[ysb, ident], writes=[pb], signal=(k == 7))
                    if half == 0:
                        fw.op("dve", lambda e: e.tensor_copy(yT.ap[:, half * 8:(half + 1) * 8, :],
                                                             bf(pb).rearrange("p (k n) -> p k n", k=8)),
                              reads=[pb], writes=[yT])
                    else:
                        fw.op("act", lambda e: e.activation(yT.ap[:, half * 8:(half + 1) * 8, :],
                                                            bf(pb).rearrange("p (k n) -> p k n", k=8), AF.Copy),
                              reads=[pb], writes=[yT])

            def stage_M(t):
                r0, nv = slot_rows(order, t)
                yT, hn, hs, stt = yT2[t % 2], hn2[t % 3], hs2[t % NB_], stat[t % 3]
                for half in range(2):
                    pb = bank()
                    for kc in range(16):
                        fw.op("pe", lambda e, kc=kc: e.matmul(pb.ap, yT.ap[:, kc, :],
                                                              wo.ap[:, kc, half * 512:(half + 1) * 512],
                                                              start=(kc == 0), stop=(kc == 15)),
                              reads=[yT, wo], writes=[pb], signal=(kc == 15))
                    fw.op("dve", lambda e: e.tensor_tensor(hn.ap[:, half * 512:(half + 1) * 512], pb.ap,
                                                           hs.ap[:, half * 512:(half + 1) * 512], ALU.add),
                          reads=[pb, hs], writes=[hn])
                if t + NB_ < NT:
                    loads(t + NB_)
                if not is_last or debug_h:
                    fw.dma("sp", H_d[r0:r0 + nv, :], hn.ap[0:nv, :], reads=[hn], writes=[Hb])
                if is_last or fuse_next:
                    fw.op("act", lambda e: e.activation(junk.ap, hn.ap, AF.Square, accum_out=stt.ap[:, 0:1]),
                          reads=[hn], writes=[junk, stt])
                    rstd_from_ss(stt.ap[:, 0:1], stt.ap[:, 1:2], stt.ap[:, 2:3], D_MODEL, [stt])

            def stage_X1(t):
                hn, stt = hn2[t % 3], stat[t % 3]
                if is_last:
                    ob = ob2[t % 2]
                    fw.op("dve", lambda e: e.scalar_tensor_tensor(ob.ap, hn.ap, stt.ap[:, 2:3], gF.ap,
                                                                  ALU.mult, ALU.mult),
                          reads=[hn, stt, gF], writes=[ob])
                elif fuse_next:
                    xn = xn2[t % 2]
                    fw.op("dve", lambda e: e.scalar_tensor_tensor(xn.ap, hn.ap, stt.ap[:, 2:3], gF.ap,
                                                                  ALU.mult, ALU.mult),
                          reads=[hn, stt, gF], writes=[xn])

            def stage_X2(t):
                r0, nv = slot_rows(order, t)
                if is_last:
                    ob = ob2[t % 2]
                    p_lo = max(r0, N_META)
                    if r0 + nv > p_lo:
                        fw.dma("sp", out_d[p_lo - N_META:r0 + nv - N_META, :], ob.ap[p_lo - r0:nv, :], reads=[ob])
                elif fuse_next:
                    xn = xn2[t % 2]
                    pb = bank()
                    for kc in range(8):
                        fw.op("pe", lambda e, kc=kc: e.transpose(bf(pb)[:, kc * 128:(kc + 1) * 128],
                                                                 xn.ap[:, kc * 128:(kc + 1) * 128], ident.ap),
                              reads=[xn, ident], writes=[pb], signal=(kc == 7))
                    pv3 = bf(pb).rearrange("p (k n) -> p k n", k=8)
                    for (plo, phi, c0) in next_cols(order, t):
                        fw.op("act", lambda e, plo=plo, phi=phi, c0=c0: e.activation(
                            xnT.ap[:, :, c0:c0 + (phi - plo)], pv3[:, :, plo:phi], AF.Copy),
                            reads=[pb], writes=[xnT])

            for t0 in range(NB_):
                loads(t0)
            stage_T(0)
            for t in range(NT):
                if t >= 2:
                    stage_X1(t - 2)
                if t + 1 < NT:
                    stage_T(t + 1)
                stage_M(t)
                if t >= 2:
                    stage_X2(t - 2)
            for t in (NT - 2, NT - 1):
                stage_X1(t)
                stage_X2(t)

        def gla_mixer(li, gi):
            w_in = gwin_d[gi]
            WC = 1552
            NSL = 2
            wg = sb([17, 1024], F32, "wg")
            fw.dma("sp", wg.ap, gwg_d[gi], writes=[wg])
            gn = sb([128, D_INNER], F32, "gn")
            fw.dma("sp", gn.ap, ggn_d[gi:gi + 1, :].partition_broadcast(128), writes=[gn])
            qscale = 256.0 ** -0.5

            class Slot:
                pass

            slots = []
            for _ in range(NSL):
                sl = Slot()
                sl.W = sb([128, 8, WC], BF16, "Wh")
                sl.glrT = sb([17, 128], F32, "glrT")
                fw.op("dve", lambda e: e.memset(sl.glrT.ap, 1.0), writes=[sl.glrT])
                sl.S = sb([128, 2, 512], F32, "S")
                sl.Sb = sb([128, 2, 512], BF16, "Sb")
                sl.junk = sb([128, 512], BF16, "gjunk")
                mk = lambda shape, dt, nm: [sb(shape, dt, nm) for _ in range(2)]
                sl.elg = mk([128, 256], F32, "elg")
                sl.EB = mk([128, 2, 128], F32, "EB")
                sl.ENB = mk([128, 2, 128], F32, "ENB")
                sl.ES = mk([128, 256], F32, "ES")
                sl.v = mk([128, 512], BF16, "v")
                sl.ez = mk([128, 512], F32, "ez")
                sl.zs = mk([128, 512], BF16, "zs")
                sl.kd = mk([128, 256], BF16, "kd")
                sl.qe = mk([128, 2, 128], BF16, "qe")
                sl.ke = mk([128, 2, 128], BF16, "ke")
                sl.at = mk([128, 128], BF16, "at")
                sl.on = mk([128, 512], F32, "on")
                sl.y = mk([128, 512], BF16, "y")
                sl.gst = mk([128, 4], F32, "gst")
                slots.append(sl)

            def load_w(h, sl):
                for kc4 in range(0, 8, 4):
                    fw.dma("pool", sl.W.ap[:, kc4:kc4 + 4, :],
                           w_in[h][kc4 * 128:(kc4 + 4) * 128, :].rearrange("(kc p) n -> p kc n", p=128),
                           writes=[sl.W])

            def head_gen(h, sl):
                W, glrT, S, Sb = sl.W, sl.glrT, sl.S, sl.Sb
                pend = []
                fw.op("pool", lambda e: e.memset(S.ap, 0.0), writes=[S])
                fw.op("pool", lambda e: e.memset(Sb.ap, 0.0), writes=[Sb])
                for it, t in enumerate([32] + list(range(32))):
                    i2 = it % 2
                    Um, Lmm = (UM, LMM) if t == 32 else (U, LM)
                    xt = lambda kc: xnT.ap[:, kc, t * 128:(t + 1) * 128]
                    elg, EB, ENB, ES = sl.elg[i2], sl.EB[i2], sl.ENB[i2], sl.ES[i2]
                    vb, ez, zs, kd, qe, ke, at, on, y, gst = (sl.v[i2], sl.ez[i2], sl.zs[i2], sl.kd[i2], sl.qe[i2],
                                                              sl.ke[i2], sl.at[i2], sl.on[i2], sl.y[i2], sl.gst[i2])
                    pg = bank()
                    for kc in range(8):
                        fw.op("pe", lambda e, kc=kc: e.matmul(pg.ap[0:16, 0:128], W.ap[:, kc, 1536:1552], xt(kc),
                                                              start=(kc == 0), stop=(kc == 7)),
                              reads=[W, xnT], writes=[pg], signal=(kc == 7))
                    fw.op("dve", lambda e: e.tensor_copy(glrT.ap[0:16, :], pg.ap[0:16, 0:128]),
                          reads=[pg], writes=[glrT])
                    pv = bank()
                    for kc in range(8):
                        fw.op("pe", lambda e, kc=kc: e.matmul(pv.ap, xt(kc), W.ap[:, kc, 512:1024],
                                                              start=(kc == 0), stop=(kc == 7)),
                              reads=[W, xnT], writes=[pv], signal=(kc == 7))
                    fw.op("act", lambda e: e.activation(vb.ap, pv.ap, AF.Copy), reads=[pv], writes=[vb])
                    yield
                    while pend:
                        pend.pop(0)()
                    pl = bank()
                    fw.op("pe", lambda e: e.matmul(pl.ap[:, 0:256], glrT.ap, wg.ap[:, 256 * h:256 * (h + 1)],
                                                   start=True, stop=True),
                          reads=[glrT, wg], writes=[pl])
                    fw.op("act", lambda e: e.activation(elg.ap, pl.ap[:, 0:256], AF.Exp, scale=-1.0),
                          reads=[pl], writes=[elg])
                    fw.op("act", lambda e: e.activation(elg.ap, elg.ap, AF.Ln, bias=1.0), reads=[elg], writes=[elg])
                    pz = bank()
                    for kc in range(8):
                        fw.op("pe", lambda e, kc=kc: e.matmul(pz.ap, xt(kc), W.ap[:, kc, 1024:1536],
                                                              start=(kc == 0), stop=(kc == 7)),
                              reads=[W, xnT], writes=[pz], signal=(kc == 7))
                    fw.op("act", lambda e: e.activation(ez.ap, pz.ap, AF.Exp, scale=-1.0), reads=[pz], writes=[ez])
                    fw.op("act", lambda e: e.activation(ez.ap, ez.ap, AF.Ln, bias=1.0), reads=[ez], writes=[ez])
                    fw.op("act", lambda e: e.activation(ez.ap, ez.ap, AF.Exp, scale=-1.0), reads=[ez], writes=[ez])
                    fw.op("dve", lambda e: e.tensor_tensor(zs.ap, pz.ap, ez.ap, ALU.mult), reads=[pz, ez], writes=[zs])
                    yield
                    pc = bank()
                    for dc in range(2):
                        fw.op("pe", lambda e, dc=dc: e.matmul(pc.ap[:, dc * 128:(dc + 1) * 128],
                                                              elg.ap[:, dc * 128:(dc + 1) * 128], Um,
                                                              start=True, stop=True),
                              reads=[elg, cst], writes=[pc], signal=False)
                    fw.op("pe", lambda e: e.matmul(pc.ap[:, 256:512], Lmm, elg.ap, start=True, stop=True),
                          reads=[elg, cst], writes=[pc])
                    pcv = pc.ap[:, 0:256].rearrange("p (c n) -> p c n", c=2)
                    fw.op("act", lambda e: e.activation(EB.ap, pcv, AF.Exp, scale=-1.0 / 16), reads=[pc], writes=[EB])
                    fw.op("act", lambda e: e.activation(ENB.ap, pcv, AF.Exp, scale=1.0 / 16), reads=[pc], writes=[ENB])
                    fw.op("act", lambda e: e.activation(ES.ap, pc.ap[:, 256:512], AF.Exp, scale=-1.0 / 16),
                          reads=[pc], writes=[ES])
                    pk = bank()
                    for kc in range(8):
                        fw.op("pe", lambda e, kc=kc: e.matmul(pk.ap[:, 0:256], xt(kc), W.ap[:, kc, 256:512],
                                                              start=(kc == 0), stop=(kc == 7)),
                              reads=[W, xnT], writes=[pk], signal=(kc == 7))
                    fw.op("dve", lambda e: e.tensor_tensor(kd.ap, pk.ap[:, 0:256], ES.ap, ALU.mult),
                          reads=[pk, ES], writes=[kd])
                    pq = bank()
                    for blk in range(4):
                        for kc in range(8):
                            fw.op("pe", lambda e, kc=kc, blk=blk: e.matmul(pq.ap[:, blk * 128:(blk + 1) * 128],
                                                                           W.ap[:, kc, blk * 128:(blk + 1) * 128], xt(kc),
                                                                           start=(kc == 0), stop=(kc == 7)),
                                  reads=[W, xnT], writes=[pq], signal=(kc == 7 and blk == 3))
                    fw.op("dve", lambda e: e.scalar_tensor_tensor(qe.ap, pq.ap[:, 0:256].rearrange("p (c n) -> p c n", c=2),
                                                                  qscale, EB.ap, ALU.mult, ALU.mult),
                          reads=[pq, EB], writes=[qe])
                    fw.op("dve", lambda e: e.tensor_tensor(ke.ap, pq.ap[:, 256:512].rearrange("p (c n) -> p c n", c=2),
                                                           ENB.ap, ALU.mult),
                          reads=[pq, ENB], writes=[ke])
                    yield
                    pa = bank()
                    for dc in range(2):
                        fw.op("pe", lambda e, dc=dc: e.matmul(pa.ap[:, 0:128], ke.ap[:, dc, :], qe.ap[:, dc, :],
                                                              start=(dc == 0), stop=(dc == 1)),
                              reads=[ke, qe], writes=[pa], signal=(dc == 1))
                    fw.op("dve", lambda e: e.tensor_tensor(at.ap, pa.ap[:, 0:128], Um, ALU.mult),
                          reads=[pa, cst], writes=[at])
                    yield
                    po = bank()
                    fw.op("pe", lambda e: e.matmul(po.ap, at.ap, vb.ap, start=True, stop=False),
                          reads=[at, vb], writes=[po], signal=False)
                    for dc in range(2):
                        fw.op("pe", lambda e, dc=dc: e.matmul(po.ap, qe.ap[:, dc, :], Sb.ap[:, dc, :],
                                                              start=False, stop=(dc == 1)),
                              reads=[qe, Sb], writes=[po], signal=(dc == 1))
                    fw.op("act", lambda e: e.activation(on.ap, po.ap, AF.Copy), reads=[po], writes=[on])
                    for dc in range(2):
                        pS = bank()
                        fw.op("pe", lambda e, dc=dc, pS=pS: e.matmul(pS.ap, kd.ap[:, dc * 128:(dc + 1) * 128], vb.ap,
                                                                     start=True, stop=True),
                              reads=[kd, vb], writes=[pS])
                        fw.op("dve", lambda e, dc=dc, pS=pS: e.scalar_tensor_tensor(S.ap[:, dc, :], S.ap[:, dc, :],
                                                                                    EB.ap[:, dc, 127:128], pS.ap,
                                                                                    ALU.mult, ALU.add),
                              reads=[S, EB, pS], writes=[S])
                    fw.op("act", lambda e: e.activation(Sb.ap, S.ap, AF.Copy), reads=[S], writes=[Sb])

                    def tail(on=on, gst=gst, zs=zs, y=y, t=t):
                        fw.op("act", lambda e: e.activation(sl.junk.ap, on.ap, AF.Square, accum_out=gst.ap[:, 0:1]),
                              reads=[on], writes=[sl.junk, gst])
                        rstd_from_ss(gst.ap[:, 0:1], gst.ap[:, 1:2], gst.ap[:, 2:3], 512, [gst])
                        fw.op("dve", lambda e: e.scalar_tensor_tensor(on.ap, on.ap, gst.ap[:, 2:3],
                                                                      gn.ap[:, 512 * h:512 * (h + 1)],
                                                                      ALU.mult, ALU.mult),
                              reads=[on, gst, gn], writes=[on])
                        fw.op("pool", lambda e: e.tensor_tensor(y.ap, on.ap, zs.ap, ALU.mult),
                              reads=[on, zs], writes=[y])
                        fw.dma("sp", Y_d[t * 128:(t + 1) * 128, 512 * h:512 * (h + 1)], y.ap, reads=[y], writes=[Yb])
                    pend.append(tail)
                    yield
                while pend:
                    pend.pop(0)()
                yield

            for pair in range(0, 4, NSL):
                for k in range(NSL):
                    load_w(pair + k, slots[k])
                gens = [head_gen(pair + k, slots[k]) for k in range(NSL)]
                while gens:
                    for g in list(gens):
                        try:
                            next(g)
                        except StopIteration:
                            gens.remove(g)

        def diff_mixer(li, di, w_prefetched=False):
            w_in = dwin_d[di]
            lam_init = 0.8 - 0.6 * math.exp(-0.3 * li)
            base_mark["diff"] = nc.sbuf_bytes_remaining
            if w_prefetched:
                assert base_mark.get("wnext") == nc.sbuf_bytes_remaining, "w_in prefetch aliasing broken"
            Wh2 = [sb([128, 8, 1024], BF16, "Wd") for _ in range(2)]
            KT = sb([128, 2, LP], BF16, "KT")
            QT = sb([128, 2, LP], BF16, "QT")
            V = sb([128, NT, 258], BF16, "V")
            ZS = sb([128, NT, 256], BF16, "ZS")
            fw.op("pool", lambda e: e.memset(V.ap, 1.0), writes=[V])
            relE = sb([128, 3, 256], F32, "relE")
            Ehi2 = [sb([128, 3, 256], BF16, "Ehi") for _ in range(2)]
            Elo2 = [sb([128, 3, 256], BF16, "Elo") for _ in range(2)]
            cv = sb([128, 8], F32, "cv")
            fw.dma("sp", cv.ap, relb_d[31:32, :].partition_broadcast(128), writes=[cv])
            lv = sb([128, 512], F32, "lv")
            fw.dma("sp", lv.ap, dlam_d[di:di + 1, :].partition_broadcast(128), writes=[lv])
            lt = sb([128, 8], F32, "lt")
            lj = sb([128, 128], F32, "lj")
            for c in range(2):
                fw.op("dve", lambda e, c=c: e.tensor_tensor(lj.ap, lv.ap[:, 256 * c:256 * c + 128],
                                                            lv.ap[:, 256 * c + 128:256 * c + 256], ALU.mult),
                      reads=[lv], writes=[lj])
                fw.op("act", lambda e, c=c: e.activation(lj.ap, lj.ap, AF.Copy, accum_out=lt.ap[:, c:c + 1]),
                      reads=[lj], writes=[lj, lt])
            fw.op("act", lambda e: e.activation(lt.ap[:, 2:4], lt.ap[:, 0:2], AF.Exp), reads=[lt], writes=[lt])
            fw.op("dve", lambda e: e.tensor_tensor(lt.ap[:, 4:5], lt.ap[:, 3:4], lt.ap[:, 2:3], ALU.subtract),
                  reads=[lt], writes=[lt])
            fw.op("dve", lambda e: e.tensor_scalar(lt.ap[:, 5:6], lt.ap[:, 4:5], -lam_init, None, ALU.add),
                  reads=[lt], writes=[lt])
            nlam = lt.ap[:, 5:6]
            gnd = sb([128, 256], F32, "gnd")
            fw.dma("sp", gnd.ap, dgn_d[di:di + 1, :].partition_broadcast(128), writes=[gnd])
            fw.op("dve", lambda e: e.tensor_scalar(gnd.ap, gnd.ap, 0.5 * (1.0 - lam_init), None, ALU.mult),
                  reads=[gnd], writes=[gnd])
            ez2 = [sb([128, 256], F32, "dez") for _ in range(2)]
            PT4 = [sb([128, 2, 256], BF16, "PT") for _ in range(4)]
            cp3 = [sb([128, 2, 258], F32, "cpO") for _ in range(4)]
            y3 = [sb([128, 256], BF16, "dy") for _ in range(4)]
            st3 = [sb([128, 8], F32, "dst") for _ in range(4)]
            junkf = sb([128, 256], F32, "junkf")
            scale = 128.0 ** -0.5
            cnt = {"pt": 0, "tmp": 0, "ep": 0, "ez": 0}

            def load_w(h):
                W = Wh2[h % 2]
                for kc4 in range(0, 8, 4):
                    fw.dma("pool", W.ap[:, kc4:kc4 + 4, :],
                           w_in[h][kc4 * 128:(kc4 + 4) * 128, :].rearrange("(kc p) n -> p kc n", p=128),
                           writes=[W])

            if not w_prefetched:
                load_w(0)
            for h in range(8):
                if h + 1 < 8 and not (w_prefetched and h == 0):
                    load_w(h + 1)
                W = Wh2[h % 2]
                Ehi, Elo = Ehi2[h % 2], Elo2[h % 2]
                fw.dma("sp", relE.ap, relE_d[h], writes=[relE])
                fw.op("dve", lambda e: e.tensor_scalar(relE.ap, relE.ap, 1.0 / scale, None, ALU.mult),
                      reads=[relE], writes=[relE])
                fw.op("dve", lambda e: e.tensor_copy(Ehi.ap, relE.ap), reads=[relE], writes=[Ehi])
                fw.op("dve", lambda e: e.tensor_tensor(Elo.ap, relE.ap, Ehi.ap, ALU.subtract),
                      reads=[relE, Ehi], writes=[Elo])
                for blk in range(4):
                    dst = QT if blk < 2 else KT
                    p = blk % 2
                    for s in range(9):
                        s0 = 512 * s
                        w = min(512, LP - s0)
                        pb = bank()
                        for kc in range(8):
                            fw.op("pe", lambda e, kc=kc: e.matmul(pb.ap[:, 0:w], W.ap[:, kc, blk * 128:(blk + 1) * 128],
                                                                  xnT.ap[:, kc, s0:s0 + w],
                                                                  start=(kc == 0), stop=(kc == 7)),
                                  reads=[W, xnT], writes=[pb], signal=(kc == 7))
                        if (s + blk) % 2 == 0:
                            fw.op("act", lambda e: e.activation(dst.ap[:, p, s0:s0 + w], pb.ap[:, 0:w], AF.Copy),
                                  reads=[pb], writes=[dst])
                        else:
                            fw.op("dve", lambda e: e.tensor_copy(dst.ap[:, p, s0:s0 + w], pb.ap[:, 0:w]),
                                  reads=[pb], writes=[dst])
                for t in range(NT):
                    pb = bank()
                    for kc in range(8):
                        fw.op("pe", lambda e, kc=kc: e.matmul(pb.ap, xnT.ap[:, kc, t * 128:(t + 1) * 128],
                                                              W.ap[:, kc, 512:1024], start=(kc == 0), stop=(kc == 7)),
                              reads=[W, xnT], writes=[pb], signal=(kc == 7))
                    ez = ez2[cnt["ez"] % 2]
                    cnt["ez"] += 1
                    fw.op("act", lambda e: e.activation(V.ap[:, t, 0:256], pb.ap[:, 0:256], AF.Copy),
                          reads=[pb], writes=[V])
                    fw.op("act", lambda e: e.activation(ez.ap, pb.ap[:, 256:512], AF.Tanh, scale=0.5),
                          reads=[pb], writes=[ez])
                    fw.op("dve", lambda e: e.scalar_tensor_tensor(ZS.ap[:, t, :], ez.ap, 1.0, pb.ap[:, 256:512],
                                                                  ALU.add, ALU.mult),
                          reads=[pb, ez], writes=[ZS])
                if h == 7:
                    for half in range(2):
                        fw.dma("pool", Wh2[half].ap,
                               dwo_d[di][half * 1024:(half + 1) * 1024, :].rearrange("(kc p) n -> p kc n", p=128),
                               writes=[Wh2[half]])
                Ob = [[banks[2 * p + sub] for sub in range(2)] for p in range(2)]
                steps = []
                for I in range(17):
                    nk = min(2 * I + 2, NT)
                    for j in range(nk):
                        steps.append((I, j, j == nk - 1))
                sbank = {}

                def emit_qk(si):
                    I, j, _ = steps[si]
                    Wq = 256 if I < 16 else 128
                    q0 = 256 * I
                    pS = bank(4, 8)
                    sbank[si] = pS
                    Sv = pS.ap.rearrange("p (c n) -> p c n", c=2)
                    kind = j - (2 * I - 1)
                    c0 = 128 if kind == 2 else 0
                    for p in range(2):
                        fw.op("pe", lambda e, p=p: e.matmul(Sv[:, p, c0:Wq], KT.ap[:, p, j * 128:(j + 1) * 128],
                                                            QT.ap[:, p, q0 + c0:q0 + Wq], start=True, stop=(kind < 0)),
                              reads=[KT, QT], writes=[pS], signal=(p == 1 and kind < 0))
                        if kind >= 0:
                            fw.op("pe", lambda e, p=p: e.matmul(Sv[:, p, c0:Wq], ident.ap, Ehi.ap[:, kind, c0:Wq],
                                                                start=False, stop=False),
                                  reads=[ident, Ehi], writes=[pS], signal=False)
                            fw.op("pe", lambda e, p=p: e.matmul(Sv[:, p, c0:Wq], ident.ap, Elo.ap[:, kind, c0:Wq],
                                                                start=False, stop=True),
                                  reads=[ident, Elo], writes=[pS], signal=(p == 1))

                def emit_rest(si):
                    I, j, last_step = steps[si]
                    Wq = 256 if I < 16 else 128
                    nsub = Wq // 128
                    pS = sbank.pop(si)
                    Sv = pS.ap.rearrange("p (c n) -> p c n", c=2)
                    PT = PT4[cnt["pt"] % 4]
                    cnt["pt"] += 1
                    kind = j - (2 * I - 1)
                    if kind < 0:
                        fw.op("act", lambda e: e.activation(PT.ap[:, :, 0:Wq], Sv[:, :, 0:Wq], AF.Exp,
                                                            scale=scale, bias=cv.ap[:, h:h + 1]),
                              reads=[pS, cv], writes=[PT])
                    else:
                        c0 = 128 if kind == 2 else 0
                        fw.op("act", lambda e: e.activation(PT.ap[:, :, c0:Wq], Sv[:, :, c0:Wq], AF.Exp, scale=scale),
                              reads=[pS], writes=[PT])
                    mm = []
                    for p in range(2):
                        for sub in range(nsub):
                            if kind == 2 and sub == 0:
                                continue
                            mm.append((p, sub, j == 2 * I + sub))
                    for n_, (p, sub, last) in enumerate(mm):
                        fw.op("pe", lambda e, p=p, sub=sub, last=last: e.matmul(
                            Ob[p][sub].ap[:, 0:257], PT.ap[:, p, sub * 128:(sub + 1) * 128], V.ap[:, j, 0:257],
                            start=(j == 0), stop=last),
                            reads=[PT, V], writes=[Ob[p][sub]], signal=(n_ == len(mm) - 1))
                    for sub in range(nsub):
                        if j != 2 * I + sub:
                            continue
                        qi = 2 * I + sub
                        k3 = cnt["ep"] % 4
                        cnt["ep"] += 1
                        cp, y, dst_ = cp3[k3], y3[k3], st3[k3]
                        O1, O2 = Ob[0][sub], Ob[1][sub]
                        fw.op("dve", lambda e: e.tensor_copy(cp.ap[:, 0, 0:257], O1.ap[:, 0:257]), reads=[O1], writes=[cp])
                        fw.op("dve", lambda e: e.tensor_copy(cp.ap[:, 1, 0:257], O2.ap[:, 0:257]), reads=[O2], writes=[cp])

                        def stage_b(cp=cp, dst_=dst_):
                            fw.op("dve", lambda e: e.reciprocal(dst_.ap[:, 0:2], cp.ap[:, :, 256]), reads=[cp], writes=[dst_])
                            fw.op("dve", lambda e: e.tensor_tensor(dst_.ap[:, 2:3], dst_.ap[:, 1:2], nlam, ALU.mult),
                                  reads=[dst_, lt], writes=[dst_])
                            fw.op("dve", lambda e: e.tensor_scalar(cp.ap[:, 0, 0:256], cp.ap[:, 0, 0:256],
                                                                   dst_.ap[:, 0:1], None, ALU.mult),
                                  reads=[cp, dst_], writes=[cp])
                            fw.op("dve", lambda e: e.scalar_tensor_tensor(cp.ap[:, 1, 0:256], cp.ap[:, 1, 0:256],
                                                                          dst_.ap[:, 2:3], cp.ap[:, 0, 0:256],
                                                                          ALU.mult, ALU.add),
                                  reads=[cp, dst_], writes=[cp])
                            fw.op("dve", lambda e: e.scalar_tensor_tensor(junkf.ap, cp.ap[:, 1, 0:256], 1.0,
                                                                          cp.ap[:, 1, 0:256], ALU.mult, ALU.mult,
                                                                          accum_out=dst_.ap[:, 3:4]),
                                  reads=[cp], writes=[junkf, dst_])
                            rstd_from_ss(dst_.ap[:, 3:4], dst_.ap[:, 4:5], dst_.ap[:, 5:6], 256, [dst_])

                        def stage_c(cp=cp, dst_=dst_, y=y, qi=qi):
                            fw.op("dve", lambda e: e.scalar_tensor_tensor(cp.ap[:, 0, 0:256], cp.ap[:, 1, 0:256],
                                                                          dst_.ap[:, 5:6], gnd.ap, ALU.mult, ALU.mult),
                                  reads=[cp, dst_, gnd], writes=[cp])
                            fw.op("pool", lambda e: e.tensor_tensor(y.ap, cp.ap[:, 0, 0:256], ZS.ap[:, qi, :], ALU.mult),
                                  reads=[cp, ZS], writes=[y])
                            fw.dma("sp", Y_d[qi * 128:(qi + 1) * 128, 256 * h:256 * (h + 1)], y.ap,
                                   reads=[y], writes=[Yb])

                        deferred.append((si + 1, stage_b))
                        deferred.append((si + 4, stage_c))
                    deferred.sort(key=lambda d: d[0])
                    while deferred and deferred[0][0] <= si:
                        deferred.pop(0)[1]()

                deferred = []
                LA = 3
                nq = 0
                for si in range(len(steps)):
                    while nq < min(si + 1 + LA, len(steps)):
                        emit_qk(nq)
                        nq += 1
                    emit_rest(si)
                deferred.sort(key=lambda d: d[0])
                while deferred:
                    deferred.pop(0)[1]()

        last_layer = layers[-1]
        fw.op("pool", lambda e: e.memset(xnT.ap[:, :, LTOK:LP], 0.0), writes=[xnT])
        for n_l, li in enumerate(layers):
            is_gla = (li % 2 == 0)
            order = "real" if is_gla else "pos"
            if n_l == 0:
                fw.barrier()
                with ExitStack() as s1:
                    cur[0] = s1
                    phase_A(li, order)
                    fw.barrier()
            with ExitStack() as s2:
                cur[0] = s2
                if is_gla:
                    gla_mixer(li, li // 2)
                else:
                    diff_mixer(li, li // 2, w_prefetched=(n_l > 0))
                fw.barrier()
            with ExitStack() as s3:
                cur[0] = s3
                is_last = final and li == last_layer
                phase_D(li, order, gwo_d[li // 2] if is_gla else dwo_d[li // 2], is_last,
                        fuse_next=(li != last_layer), wo_prefetched=(not is_gla),
                        next_diff=((li + 1) // 2 if (is_gla and li != last_layer) else None))
                fw.barrier()
            cur[0] = st
        fw.finish()
        build_program.stats = (fw.n_inst, fw.n_wait, {e.name: e.cnt for e in fw.E.values()})
    return nc


def _t5_bucket_np(n):
    n = np.asarray(n, dtype=np.int64)
    nf = np.maximum(n, 1).astype(np.float32)
    large = 16 + (np.log(nf / np.float32(16)) / np.float32(math.log(128 / 16)) * np.float32(16)).astype(np.int32)
    large = np.minimum(large, 31)
    return np.where(n < 16, n, large)


def _rel_tiles(rel_bias):
    kl = np.arange(128)[:, None]
    ql = np.arange(128)[None, :]
    d0 = ql - kl
    d1 = 128 + ql - kl
    b0 = _t5_bucket_np(np.maximum(d0, 0))
    b1 = _t5_bucket_np(d1)
    out = np.empty((8, 128, 3, 256), np.float32)
    for h in range(8):
        D0 = np.where(d0 >= 0, rel_bias[b0, h], np.float32(NEG)).astype(np.float32)
        D1 = rel_bias[b1, h].astype(np.float32)
        C = np.full((128, 128), rel_bias[31, h], np.float32)
        M = np.full((128, 128), NEG, np.float32)
        out[h, :, 0, :] = np.concatenate([D1, C], 1)
        out[h, :, 1, :] = np.concatenate([D0, D1], 1)
        out[h, :, 2, :] = np.concatenate([M, D0], 1)
    return out


def _consts():
    j = np.arange(128)[:, None]
    i = np.arange(128)[None, :]
    c = np.zeros((128, 5, 128), np.float32)
    c[:, 0, :] = np.eye(128, dtype=np.float32)
    c[:, 1, :] = (j <= i)
    c[:, 2, :] = (j > i)
    c[:, 3, :] = (j <= i) & (j < N_META)
    c[:, 4, :] = (j > i) & (j < N_META)
    return c


def _gla_w_layout(w):
    out = np.empty((2, 4, D_MODEL, 1552), np.float32)
    for h in range(4):
        out[:, h, :, 0:256] = w[:, :, 256 * h:256 * (h + 1)]
        out[:, h, :, 256:512] = w[:, :, 1024 + 256 * h:1024 + 256 * (h + 1)]
        out[:, h, :, 512:1024] = w[:, :, 2048 + 512 * h:2048 + 512 * (h + 1)]
        out[:, h, :, 1024:1536] = w[:, :, 4096 + 512 * h:4096 + 512 * (h + 1)]
        out[:, h, :, 1536:1552] = w[:, :, 6144:6160]
    return out


def _diff_w_layout(w):
    out = np.empty((2, 8, D_MODEL, 1024), np.float32)
    for h in range(8):
        for j in range(4):
            out[:, h, :, 256 * j:256 * (j + 1)] = w[:, :, 2048 * j + 256 * h:2048 * j + 256 * (h + 1)]
    return out


_NC_CACHE = {}


def kernel(x, meta, g_norm, gla_w_in, gla_w_gate_up, gla_b_gate, gla_gnorm, gla_w_out,
           diff_w_in, diff_lam, diff_gnorm, diff_w_out, rel_bias, g_final):
    f = lambda a: np.ascontiguousarray(np.asarray(a, dtype=np.float32))
    x = f(x)
    rel_bias = f(rel_bias)
    shared = {
        "meta": f(meta), "g_norm": f(g_norm), "g_final": f(g_final).reshape(1, D_MODEL),
        "gla_w_in": _gla_w_layout(f(gla_w_in)),
        "gla_wg": np.ascontiguousarray(np.concatenate([f(gla_w_gate_up), f(gla_b_gate)[:, None, :]], axis=1)),
        "gla_gnorm": f(gla_gnorm), "gla_w_out": f(gla_w_out),
        "diff_w_in": _diff_w_layout(f(diff_w_in)), "diff_lam": f(diff_lam).reshape(2, 512), "diff_gnorm": f(diff_gnorm),
        "diff_w_out": f(diff_w_out), "relE": _rel_tiles(rel_bias), "rel_bias": rel_bias, "consts": _consts(),
    }
    if "nc" not in _NC_CACHE:
        _NC_CACHE["nc"] = build_program()
    nc = _NC_CACHE["nc"]
    in_maps = [dict(shared, x=np.ascontiguousarray(x[b])) for b in range(8)]
    res = run_bass_kernel_spmd(nc, in_maps, core_ids=list(range(8)))
    return np.stack([np.asarray(r["out"], dtype=np.float32) for r in res.results], axis=0)
```

```python
import math
from contextlib import ExitStack

import numpy as np
import concourse.bass as bass
import concourse.mybir as mybir
from concourse.bass_utils import run_bass_kernel_spmd

F32 = mybir.dt.float32
BF16 = mybir.dt.bfloat16
AF = mybir.ActivationFunctionType
ALU = mybir.AluOpType

D_MODEL = 1024
SEQ = 4096
N_META = 16
LTOK = SEQ + N_META
NT = 33
LP = NT * 128
D_INNER = 2048
GLA_PROJ = 6160
DIFF_PROJ = 8192
EPS = 1e-6
NEG = -30000.0


class Buf:
    __slots__ = ("name", "w", "r", "ap")

    def __init__(self, name, ap=None):
        self.name = name
        self.w = None
        self.r = []
        self.ap = ap


class Eng:
    def __init__(self, name, h, same_sync):
        self.name = name
        self.h = h
        self.sem = None
        self.semkey = None
        self.cnt = 0
        self.seen = {}
        self.same_sync = same_sync
        self.pending = False


def _compress(lst):
    m = {}
    for k, v in lst:
        if m.get(k, 0) < v:
            m[k] = v
    return list(m.items())


class FW:
    def __init__(self, nc, n_dma_sems=12):
        self.nc = nc
        self.sems = {}
        self.E = {
            "pe": Eng("pe", nc.tensor, False),
            "act": Eng("act", nc.scalar, True),
            "dve": Eng("dve", nc.vector, True),
            "pool": Eng("pool", nc.gpsimd, True),
            "sp": Eng("sp", nc.sync, True),
        }
        self._stack = None
        self.epoch = 0
        self.n_dma_sems = n_dma_sems
        self.dma_pool = {}
        self.dma_rr = {}
        self.bufs = []
        self.n_inst = 0
        self.n_wait = 0

    def _new_sem(self, key):
        h = self._stack.enter_context(self.nc.semaphore(key))
        self.sems[key] = h
        return h

    def start(self, stack):
        self._stack = stack
        for e in self.E.values():
            e.semkey = f"s_{e.name}_{self.epoch}"
            e.sem = self._new_sem(e.semkey)
        for q in ("sp", "pool"):
            self.dma_pool[q] = []
            for i in range(self.n_dma_sems):
                k = f"d_{q}_{i}"
                self._new_sem(k)
                self.dma_pool[q].append([k, 0])
            self.dma_rr[q] = 0

    def buf(self, name, ap=None):
        b = Buf(name, ap)
        self.bufs.append(b)
        return b

    def _wait(self, eng, key, val):
        if eng.seen.get(key, 0) >= val:
            return
        if key == eng.semkey:
            if not eng.same_sync:
                return
            assert val <= eng.cnt, f"{eng.name} self-wait on pending value"
        else:
            for o in self.E.values():
                if o.semkey == key:
                    assert val <= o.cnt, f"{eng.name} waits on pending {o.name} {val}>{o.cnt}"
        eng.h.wait_ge(self.sems[key], val)
        eng.seen[key] = val
        self.n_wait += 1

    @staticmethod
    def _deps(reads, writes):
        deps = {}

        def add(d):
            if d is None:
                return
            k, v = d
            if deps.get(k, 0) < v:
                deps[k] = v
        for b in reads:
            add(b.w)
        for b in writes:
            add(b.w)
            for d in b.r:
                add(d)
        return deps

    def op(self, en, fn, reads=(), writes=(), signal=True):
        eng = self.E[en]
        for k, v in self._deps(reads, writes).items():
            self._wait(eng, k, v)
        inst = fn(eng.h)
        self.n_inst += 1
        if signal:
            inst.then_inc(eng.sem, 1)
            eng.cnt += 1
            idx = eng.cnt
            eng.pending = False
        else:
            idx = eng.cnt + 1
            eng.pending = True
        d = (eng.semkey, idx)
        for b in reads:
            b.r.append(d)
            if len(b.r) > 48:
                b.r = _compress(b.r)
        for b in writes:
            b.w = d
            b.r = []
        return inst

    def dma(self, q, out, in_, reads=(), writes=(), **kw):
        eng = self.E[q]
        for k, v in self._deps(reads, writes).items():
            self._wait(eng, k, v)
        pool = self.dma_pool[q]
        i = self.dma_rr[q]
        self.dma_rr[q] = (i + 1) % len(pool)
        slot = pool[i]
        if slot[1] > 0:
            self._wait(eng, slot[0], slot[1])
        inst = eng.h.dma_start(out=out, in_=in_, **kw)
        slot[1] += 16
        inst.then_inc(self.sems[slot[0]], 16)
        self.n_inst += 1
        d = (slot[0], slot[1])
        for b in reads:
            b.r.append(d)
            if len(b.r) > 48:
                b.r = _compress(b.r)
        for b in writes:
            b.w = d
            b.r = []
        return inst

    def barrier(self):
        for e in self.E.values():
            assert not e.pending, f"{e.name} pending"
        targets = [(e.semkey, e.cnt) for e in self.E.values() if e.cnt > 0]
        dma_t = [(k, c) for pool in self.dma_pool.values() for k, c in pool if c > 0]
        for e in self.E.values():
            for k, v in targets:
                if k != e.semkey:
                    self._wait(e, k, v)
            for k, v in dma_t:
                self._wait(e, k, v)
        for b in self.bufs:
            b.w = None
            b.r = []

    def finish(self):
        sp = self.E["sp"]
        for e in self.E.values():
            assert not e.pending
            if e is not sp and e.cnt > 0:
                self._wait(sp, e.semkey, e.cnt)
        for pool in self.dma_pool.values():
            for k, c in pool:
                if c > 0:
                    self._wait(sp, k, c)


def slot_rows(order, t):
    if order == "pos":
        r0 = 128 * t
        return r0, min(128, LTOK - r0)
    if t < 32:
        return N_META + 128 * t, 128
    return 0, N_META


def build_program(layers=(0, 1, 2, 3), final=True, debug_h=False):
    nc = bass.Bass("TRN2", target_bir_lowering=False)
    din = lambda n, s: nc.dram_tensor(n, list(s), F32, kind="ExternalInput").ap()
    x_d = din("x", (SEQ, D_MODEL))
    meta_d = din("meta", (N_META, D_MODEL))
    gnorm_d = din("g_norm", (4, D_MODEL))
    gfin_d = din("g_final", (1, D_MODEL))
    gwin_d = din("gla_w_in", (2, 4, D_MODEL, 1552))
    gwg_d = din("gla_wg", (2, 17, 1024))
    ggn_d = din("gla_gnorm", (2, D_INNER))
    gwo_d = din("gla_w_out", (2, D_INNER, D_MODEL))
    dwin_d = din("diff_w_in", (2, 8, D_MODEL, 1024))
    dlam_d = din("diff_lam", (2, 512))
    dgn_d = din("diff_gnorm", (2, 256))
    dwo_d = din("diff_w_out", (2, D_INNER, D_MODEL))
    relE_d = din("relE", (8, 128, 3, 256))
    relb_d = din("rel_bias", (32, 8))
    cst_d = din("consts", (128, 5, 128))
    out_d = nc.dram_tensor("out", [SEQ, D_MODEL], F32, kind="ExternalOutput").ap()
    H_d = nc.dram_tensor("Hres", [LTOK, D_MODEL], F32, kind="ExternalOutput" if debug_h else "Internal").ap()
    Y_d = nc.dram_tensor("Yscr", [LP, D_INNER], BF16, kind="Internal").ap()

    st = ExitStack()
    with st:
        fw = FW(nc)
        fw.start(st)
        _n = [0]

        cur = [st]
        base_mark = {}

        def sb(shape, dt, name=None):
            _n[0] += 1
            nm = f"{name or 't'}_{_n[0]}"
            h = cur[0].enter_context(nc.sbuf_tensor(nm, list(shape), dt))
            return fw.buf(nm, h.ap() if hasattr(h, "ap") else h[:])

        ps_all = nc.alloc_psum_tensor("ps", [128, 8, 512], F32).ap()
        banks = [fw.buf(f"bank{i}", ps_all[:, i, :]) for i in range(8)]
        rr = [0]

        def bank(lo=0, hi=8):
            while True:
                i = rr[0]
                rr[0] = (rr[0] + 1) % 8
                if lo <= i < hi:
                    return banks[i]

        def bf(b):
            return b.ap.bitcast(BF16)

        Hb = fw.buf("H_dram")
        Yb = fw.buf("Y_dram")

        cst = sb([128, 5, 128], F32, "cst")
        fw.dma("sp", cst.ap, cst_d, writes=[cst])
        ident = sb([128, 128], BF16, "ident")
        fw.dma("pool", ident.ap, cst_d[:, 0, :], writes=[ident])
        U, LM, UM, LMM = (cst.ap[:, i, :] for i in (1, 2, 3, 4))

        fw.dma("pool", H_d[0:N_META, :], meta_d, writes=[Hb])
        for c in range(8):
            fw.dma("pool", H_d[N_META + 512 * c:N_META + 512 * (c + 1), :], x_d[512 * c:512 * (c + 1), :], writes=[Hb])

        xnT = sb([128, 8, LP], BF16, "xnT")
        junk = sb([128, D_MODEL], BF16, "junk")
        eps_t = sb([128, 1], F32, "eps")
        fw.op("dve", lambda e: e.memset(eps_t.ap, EPS), writes=[eps_t])

        cpow = sb([128, 2], F32, "cpow")
        fw.op("pool", lambda e: e.memset(cpow.ap[:, 0:1], -0.5), writes=[cpow])
        fw.op("pool", lambda e: e.memset(cpow.ap[:, 1:2], -1.0), writes=[cpow])

        def rstd_from_ss(ss_ap, tmp_ap, out_ap, n, bufs):
            fw.op("pool", lambda e: e.tensor_scalar(tmp_ap, ss_ap, 1.0 / n, EPS, ALU.mult, ALU.add),
                  reads=bufs, writes=bufs)
            fw.op("pool", lambda e: e.tensor_tensor(out_ap, tmp_ap, cpow.ap[:, 0:1], ALU.pow),
                  reads=bufs + [cpow], writes=bufs)

        def phase_A(li, order, prefetch_gla=None):
            if prefetch_gla is not None:
                base_mark["gla_w"] = nc.sbuf_bytes_remaining
                Wp = [sb([128, 8, 1552], BF16, "Wp") for _ in range(2)]
                for hh in range(2):
                    for kc4 in range(0, 8, 4):
                        fw.dma("pool", Wp[hh].ap[:, kc4:kc4 + 4, :],
                               gwin_d[prefetch_gla][hh][kc4 * 128:(kc4 + 4) * 128, :].rearrange("(kc p) n -> p kc n", p=128),
                               writes=[Wp[hh]])
            hs2 = [sb([128, D_MODEL], F32, "hs") for _ in range(4)]
            stat = [sb([128, 4], F32, "stat") for _ in range(4)]
            xn2 = [sb([128, D_MODEL], BF16, "xn") for _ in range(4)]
            gA = sb([128, D_MODEL], F32, "gA")
            fw.dma("sp", gA.ap, gnorm_d[li:li + 1, :].partition_broadcast(128), writes=[gA])
            for t in range(NT):
                r0, nv = slot_rows(order, t)
                hs = hs2[t % 4]
                stt = stat[t % 4]
                xn = xn2[t % 4]
                if nv < 128:
                    fw.op("pool", lambda e: e.memset(hs.ap, 0.0), writes=[hs])
                if li == 0 and order == "real":
                    src = x_d[128 * t:128 * (t + 1), :] if t < 32 else meta_d
                    fw.dma("sp", hs.ap[0:nv, :], src, writes=[hs])
                else:
                    fw.dma("sp", hs.ap[0:nv, :], H_d[r0:r0 + nv, :], reads=[Hb], writes=[hs])
                fw.op("act", lambda e: e.activation(junk.ap, hs.ap, AF.Square, accum_out=stt.ap[:, 0:1]),
                      reads=[hs], writes=[junk, stt])
                rstd_from_ss(stt.ap[:, 0:1], stt.ap[:, 1:2], stt.ap[:, 2:3], D_MODEL, [stt])
                fw.op("dve", lambda e: e.scalar_tensor_tensor(xn.ap, hs.ap, stt.ap[:, 2:3], gA.ap,
                                                              ALU.mult, ALU.mult),
                      reads=[hs, stt, gA], writes=[xn])
                pb = bank()
                for kc in range(8):
                    fw.op("pe", lambda e, kc=kc: e.transpose(bf(pb)[:, kc * 128:(kc + 1) * 128],
                                                             xn.ap[:, kc * 128:(kc + 1) * 128], ident.ap),
                          reads=[xn, ident], writes=[pb], signal=(kc == 7))
                fw.op("dve", lambda e: e.tensor_copy(xnT.ap[:, :, t * 128:(t + 1) * 128],
                                                     bf(pb).rearrange("p (k n) -> p k n", k=8)),
                      reads=[pb], writes=[xnT])

        def next_cols(order, t):
            r0, nv = slot_rows(order, t)
            runs = []
            if order == "real":
                runs.append((0, nv, r0))
            else:
                lo = max(r0, N_META)
                if r0 < N_META:
                    runs.append((0, N_META - r0, SEQ + r0))
                if r0 + nv > lo:
                    runs.append((lo - r0, nv, lo - N_META))
            return runs

        def phase_D(li, order, w_out_d, is_last, fuse_next, wo_prefetched=False, next_diff=None):
            NB_ = 3
            if wo_prefetched:
                assert nc.sbuf_bytes_remaining == base_mark["diff"], "w_out prefetch aliasing broken"
            Wn = None
            if next_diff is not None:
                base_mark["wnext"] = nc.sbuf_bytes_remaining
                Wn = [sb([128, 8, 1024], BF16, "Wn") for _ in range(2)]
            wo = sb([128, 16, D_MODEL], BF16, "wo")
            wst = [sb([128, D_MODEL], F32, "wst") for _ in range(4)]
            if not wo_prefetched:
                for kc in range(8, 16, 4):
                    fw.dma("pool", wo.ap[:, kc:kc + 4, :],
                           w_out_d[kc * 128:(kc + 4) * 128, :].rearrange("(kc p) n -> p kc n", p=128), writes=[wo])
            for kc in range(0 if not wo_prefetched else 8, 8):
                stg = wst[kc % 4]
                fw.dma("sp", stg.ap, w_out_d[kc * 128:(kc + 1) * 128, :], writes=[stg])
                ce = ("act", "dve", "pool")[kc % 3]
                if ce == "act":
                    fw.op("act", lambda e: e.activation(wo.ap[:, kc, :], stg.ap, AF.Copy), reads=[stg], writes=[wo])
                else:
                    fw.op(ce, lambda e: e.tensor_copy(wo.ap[:, kc, :], stg.ap), reads=[stg], writes=[wo])
            if Wn is not None:
                for hh in range(2):
                    for kc4 in range(0, 8, 4):
                        fw.dma("pool", Wn[hh].ap[:, kc4:kc4 + 4, :],
                               dwin_d[next_diff][hh][kc4 * 128:(kc4 + 4) * 128, :].rearrange("(kc p) n -> p kc n", p=128),
                               writes=[Wn[hh]])
            ysb2 = [sb([128, D_INNER], BF16, "ysb") for _ in range(NB_)]
            yT2 = [sb([128, 16, 128], BF16, "yT") for _ in range(2)]
            hn2 = [sb([128, D_MODEL], F32, "hn") for _ in range(3)]
            ob2 = [sb([128, D_MODEL], F32, "ob") for _ in range(2)] if is_last else None
            hs2 = [sb([128, D_MODEL], F32, "hs") for _ in range(NB_)]
            stat = [sb([128, 4], F32, "stat") for _ in range(3)]
            gF = sb([128, D_MODEL], F32, "gF")
            xn2 = [sb([128, D_MODEL], BF16, "xnD") for _ in range(2)] if fuse_next else None
            if is_last:
                fw.dma("sp", gF.ap, gfin_d.partition_broadcast(128), writes=[gF])
            elif fuse_next:
                fw.dma("sp", gF.ap, gnorm_d[li + 1:li + 2, :].partition_broadcast(128), writes=[gF])

            def loads(t):
                r0, nv = slot_rows(order, t)
                fw.dma("sp", ysb2[t % NB_].ap, Y_d[t * 128:(t + 1) * 128, :], reads=[Yb], writes=[ysb2[t % NB_]])
                fw.dma("sp", hs2[t % NB_].ap[0:nv, :], H_d[r0:r0 + nv, :], reads=[Hb], writes=[hs2[t % NB_]])

            def stage_T(t):
                ysb, yT = ysb2[t % NB_], yT2[t % 2]
                for half in range(2):
                    pb = bank()
                    for k in range(8):
                        kc = half * 8 + k
                        fw.op("pe", lambda e, k=k, kc=kc: e.transpose(bf(pb)[:, k * 128:(k + 1) * 128],
                                                                      ysb.ap[:, kc * 128:(kc + 1) * 128], ident.ap),
                              reads=[ysb, ident], writes=[pb], signal=(k == 7))
                    if half == 0:
                        fw.op("dve", lambda e: e.tensor_copy(yT.ap[:, half * 8:(half + 1) * 8, :],
                                                             bf(pb).rearrange("p (k n) -> p k n", k=8)),
                              reads=[pb], writes=[yT])
                    else:
                        fw.op("act", lambda e: e.activation(yT.ap[:, half * 8:(half + 1) * 8, :],
                                                            bf(pb).rearrange("p (k n) -> p k n", k=8), AF.Copy),
                              reads=[pb], writes=[yT])

            def stage_M(t):
                r0, nv = slot_rows(order, t)
                yT, hn, hs, stt = yT2[t % 2], hn2[t % 3], hs2[t % NB_], stat[t % 3]
                for half in range(2):
                    pb = bank()
                    for kc in range(16):
                        fw.op("pe", lambda e, kc=kc: e.matmul(pb.ap, yT.ap[:, kc, :],
                                                              wo.ap[:, kc, half * 512:(half + 1) * 512],
                                                              start=(kc == 0), stop=(kc == 15)),
                              reads=[yT, wo], writes=[pb], signal=(kc == 15))
                    fw.op("dve", lambda e: e.tensor_tensor(hn.ap[:, half * 512:(half + 1) * 512], pb.ap,
                                                           hs.ap[:, half * 512:(half + 1) * 512], ALU.add),
                          reads=[pb, hs], writes=[hn])
                if t + NB_ < NT:
                    loads(t + NB_)
                if not is_last or debug_h:
                    fw.dma("sp", H_d[r0:r0 + nv, :], hn.ap[0:nv, :], reads=[hn], writes=[Hb])
                if is_last or fuse_next:
                    fw.op("act", lambda e: e.activation(junk.ap, hn.ap, AF.Square, accum_out=stt.ap[:, 0:1]),
                          reads=[hn], writes=[junk, stt])
                    rstd_from_ss(stt.ap[:, 0:1], stt.ap[:, 1:2], stt.ap[:, 2:3], D_MODEL, [stt])

            def stage_X1(t):
                hn, stt = hn2[t % 3], stat[t % 3]
                if is_last:
                    ob = ob2[t % 2]
                    fw.op("dve", lambda e: e.scalar_tensor_tensor(ob.ap, hn.ap, stt.ap[:, 2:3], gF.ap,
                                                                  ALU.mult, ALU.mult),
                          reads=[hn, stt, gF], writes=[ob])
                elif fuse_next:
                    xn = xn2[t % 2]
                    fw.op("dve", lambda e: e.scalar_tensor_tensor(xn.ap, hn.ap, stt.ap[:, 2:3], gF.ap,
                                                                  ALU.mult, ALU.mult),
                          reads=[hn, stt, gF], writes=[xn])

            def stage_X2(t):
                r0, nv = slot_rows(order, t)
                if is_last:
                    ob = ob2[t % 2]
                    p_lo = max(r0, N_META)
                    if r0 + nv > p_lo:
                        fw.dma("sp", out_d[p_lo - N_META:r0 + nv - N_META, :], ob.ap[p_lo - r0:nv, :], reads=[ob])
                elif fuse_next:
                    xn = xn2[t % 2]
                    pb = bank()
                    for kc in range(8):
                        fw.op("pe", lambda e, kc=kc: e.transpose(bf(pb)[:, kc * 128:(kc + 1) * 128],
                                                                 xn.ap[:, kc * 128:(kc + 1) * 128], ident.ap),
                              reads=[xn, ident], writes=[pb], signal=(kc == 7))
                    pv3 = bf(pb).rearrange("p (k n) -> p k n", k=8)
                    for (plo, phi, c0) in next_cols(order, t):
                        fw.op("act", lambda e, plo=plo, phi=phi, c0=c0: e.activation(
                            xnT.ap[:, :, c0:c0 + (phi - plo)], pv3[:, :, plo:phi], AF.Copy),
                            reads=[pb], writes=[xnT])

            for t0 in range(NB_):
                loads(t0)
            stage_T(0)
            for t in range(NT):
                if t >= 2:
                    stage_X1(t - 2)
                if t + 1 < NT:
                    stage_T(t + 1)
                stage_M(t)
                if t >= 2:
                    stage_X2(t - 2)
            for t in (NT - 2, NT - 1):
                stage_X1(t)
                stage_X2(t)

        def gla_mixer(li, gi, w_prefetched=False):
            w_in = gwin_d[gi]
            WC = 1552
            NSL = 2
            if w_prefetched:
                assert base_mark.get("gla_w") == nc.sbuf_bytes_remaining, "gla w_in prefetch aliasing broken"
            Wslots = [sb([128, 8, WC], BF16, "Wh") for _ in range(NSL)]
            wg = sb([17, 1024], F32, "wg")
            fw.dma("sp", wg.ap, gwg_d[gi], writes=[wg])
            gn = sb([128, D_INNER], F32, "gn")
            fw.dma("sp", gn.ap, ggn_d[gi:gi + 1, :].partition_broadcast(128), writes=[gn])
            qscale = 256.0 ** -0.5

            class Slot:
                pass

            slots = []
            for k_ in range(NSL):
                sl = Slot()
                sl.W = Wslots[k_]
                sl.glrT = sb([17, 128], F32, "glrT")
                fw.op("dve", lambda e: e.memset(sl.glrT.ap, 1.0), writes=[sl.glrT])
                sl.S = sb([128, 2, 512], F32, "S")
                sl.Sb = sb([128, 2, 512], BF16, "Sb")
                sl.junk = sb([128, 512], BF16, "gjunk")
                mk = lambda shape, dt, nm: [sb(shape, dt, nm) for _ in range(2)]
                sl.elg = mk([128, 256], F32, "elg")
                sl.EB = mk([128, 2, 128], F32, "EB")
                sl.ENB = mk([128, 2, 128], F32, "ENB")
                sl.ES = mk([128, 256], F32, "ES")
                sl.v = mk([128, 512], BF16, "v")
                sl.ez = mk([128, 512], F32, "ez")
                sl.zs = mk([128, 512], BF16, "zs")
                sl.kd = mk([128, 256], BF16, "kd")
                sl.qe = mk([128, 2, 128], BF16, "qe")
                sl.ke = mk([128, 2, 128], BF16, "ke")
                sl.at = mk([128, 128], BF16, "at")
                sl.on = mk([128, 512], F32, "on")
                sl.y = mk([128, 512], BF16, "y")
                sl.gst = mk([128, 4], F32, "gst")
                slots.append(sl)

            def load_w(h, sl):
                for kc4 in range(0, 8, 4):
                    fw.dma("pool", sl.W.ap[:, kc4:kc4 + 4, :],
                           w_in[h][kc4 * 128:(kc4 + 4) * 128, :].rearrange("(kc p) n -> p kc n", p=128),
                           writes=[sl.W])

            def head_gen(h, sl):
                W, glrT, S, Sb = sl.W, sl.glrT, sl.S, sl.Sb
                pend = []
                fw.op("pool", lambda e: e.memset(S.ap, 0.0), writes=[S])
                fw.op("pool", lambda e: e.memset(Sb.ap, 0.0), writes=[Sb])
                for it, t in enumerate([32] + list(range(32))):
                    i2 = it % 2
                    Um, Lmm = (UM, LMM) if t == 32 else (U, LM)
                    xt = lambda kc: xnT.ap[:, kc, t * 128:(t + 1) * 128]
                    elg, EB, ENB, ES = sl.elg[i2], sl.EB[i2], sl.ENB[i2], sl.ES[i2]
                    vb, ez, zs, kd, qe, ke, at, on, y, gst = (sl.v[i2], sl.ez[i2], sl.zs[i2], sl.kd[i2], sl.qe[i2],
                                                              sl.ke[i2], sl.at[i2], sl.on[i2], sl.y[i2], sl.gst[i2])
                    pg = bank()
                    for kc in range(8):
                        fw.op("pe", lambda e, kc=kc: e.matmul(pg.ap[0:16, 0:128], W.ap[:, kc, 1536:1552], xt(kc),
                                                              start=(kc == 0), stop=(kc == 7)),
                              reads=[W, xnT], writes=[pg], signal=(kc == 7))
                    fw.op("dve", lambda e: e.tensor_copy(glrT.ap[0:16, :], pg.ap[0:16, 0:128]),
                          reads=[pg], writes=[glrT])
                    pv = bank()
                    for kc in range(8):
                        fw.op("pe", lambda e, kc=kc: e.matmul(pv.ap, xt(kc), W.ap[:, kc, 512:1024],
                                                              start=(kc == 0), stop=(kc == 7)),
                              reads=[W, xnT], writes=[pv], signal=(kc == 7))
                    fw.op("act", lambda e: e.activation(vb.ap, pv.ap, AF.Copy), reads=[pv], writes=[vb])
                    yield
                    while pend:
                        pend.pop(0)()
                    pl = bank()
                    fw.op("pe", lambda e: e.matmul(pl.ap[:, 0:256], glrT.ap, wg.ap[:, 256 * h:256 * (h + 1)],
                                                   start=True, stop=True),
                          reads=[glrT, wg], writes=[pl])
                    fw.op("act", lambda e: e.activation(elg.ap, pl.ap[:, 0:256], AF.Exp, scale=-1.0),
                          reads=[pl], writes=[elg])
                    fw.op("act", lambda e: e.activation(elg.ap, elg.ap, AF.Ln, bias=1.0), reads=[elg], writes=[elg])
                    pz = bank()
                    for kc in range(8):
                        fw.op("pe", lambda e, kc=kc: e.matmul(pz.ap, xt(kc), W.ap[:, kc, 1024:1536],
                                                              start=(kc == 0), stop=(kc == 7)),
                              reads=[W, xnT], writes=[pz], signal=(kc == 7))
                    fw.op("act", lambda e: e.activation(ez.ap, pz.ap, AF.Exp, scale=-1.0), reads=[pz], writes=[ez])
                    fw.op("act", lambda e: e.activation(ez.ap, ez.ap, AF.Ln, bias=1.0), reads=[ez], writes=[ez])
                    fw.op("act", lambda e: e.activation(ez.ap, ez.ap, AF.Exp, scale=-1.0), reads=[ez], writes=[ez])
                    fw.op("dve", lambda e: e.tensor_tensor(zs.ap, pz.ap, ez.ap, ALU.mult), reads=[pz, ez], writes=[zs])
                    yield
                    pc = bank()
                    for dc in range(2):
                        fw.op("pe", lambda e, dc=dc: e.matmul(pc.ap[:, dc * 128:(dc + 1) * 128],
                                                              elg.ap[:, dc * 128:(dc + 1) * 128], Um,
                                                              start=True, stop=True),
                              reads=[elg, cst], writes=[pc], signal=False)
                    fw.op("pe", lambda e: e.matmul(pc.ap[:, 256:512], Lmm, elg.ap, start=True, stop=True),
                          reads=[elg, cst], writes=[pc])
                    pcv = pc.ap[:, 0:256].rearrange("p (c n) -> p c n", c=2)
                    fw.op("act", lambda e: e.activation(EB.ap, pcv, AF.Exp, scale=-1.0 / 16), reads=[pc], writes=[EB])
                    fw.op("act", lambda e: e.activation(ENB.ap, pcv, AF.Exp, scale=1.0 / 16), reads=[pc], writes=[ENB])
                    fw.op("act", lambda e: e.activation(ES.ap, pc.ap[:, 256:512], AF.Exp, scale=-1.0 / 16),
                          reads=[pc], writes=[ES])
                    pk = bank()
                    for kc in range(8):
                        fw.op("pe", lambda e, kc=kc: e.matmul(pk.ap[:, 0:256], xt(kc), W.ap[:, kc, 256:512],
                                                              start=(kc == 0), stop=(kc == 7)),
                              reads=[W, xnT], writes=[pk], signal=(kc == 7))
                    fw.op("dve", lambda e: e.tensor_tensor(kd.ap, pk.ap[:, 0:256], ES.ap, ALU.mult),
                          reads=[pk, ES], writes=[kd])
                    pq = bank()
                    for blk in range(4):
                        for kc in range(8):
                            fw.op("pe", lambda e, kc=kc, blk=blk: e.matmul(pq.ap[:, blk * 128:(blk + 1) * 128],
                                                                           W.ap[:, kc, blk * 128:(blk + 1) * 128], xt(kc),
                                                                           start=(kc == 0), stop=(kc == 7)),
                                  reads=[W, xnT], writes=[pq], signal=(kc == 7 and blk == 3))
                    fw.op("dve", lambda e: e.scalar_tensor_tensor(qe.ap, pq.ap[:, 0:256].rearrange("p (c n) -> p c n", c=2),
                                                                  qscale, EB.ap, ALU.mult, ALU.mult),
                          reads=[pq, EB], writes=[qe])
                    fw.op("dve", lambda e: e.tensor_tensor(ke.ap, pq.ap[:, 256:512].rearrange("p (c n) -> p c n", c=2),
                                                           ENB.ap, ALU.mult),
                          reads=[pq, ENB], writes=[ke])
                    yield
                    pa = bank()
                    for dc in range(2):
                        fw.op("pe", lambda e, dc=dc: e.matmul(pa.ap[:, 0:128], ke.ap[:, dc, :], qe.ap[:, dc, :],
                                                              start=(dc == 0), stop=(dc == 1)),
                              reads=[ke, qe], writes=[pa], signal=(dc == 1))
                    fw.op("dve", lambda e: e.tensor_tensor(at.ap, pa.ap[:, 0:128], Um, ALU.mult),
                          reads=[pa, cst], writes=[at])
                    yield
                    po = bank()
                    fw.op("pe", lambda e: e.matmul(po.ap, at.ap, vb.ap, start=True, stop=False),
                          reads=[at, vb], writes=[po], signal=False)
                    for dc in range(2):
                        fw.op("pe", lambda e, dc=dc: e.matmul(po.ap, qe.ap[:, dc, :], Sb.ap[:, dc, :],
                                                              start=False, stop=(dc == 1)),
                              reads=[qe, Sb], writes=[po], signal=(dc == 1))
                    fw.op("act", lambda e: e.activation(on.ap, po.ap, AF.Copy), reads=[po], writes=[on])
                    for dc in range(2):
                        pS = bank()
                        fw.op("pe", lambda e, dc=dc, pS=pS: e.matmul(pS.ap, kd.ap[:, dc * 128:(dc + 1) * 128], vb.ap,
                                                                     start=True, stop=True),
                              reads=[kd, vb], writes=[pS])
                        fw.op("dve", lambda e, dc=dc, pS=pS: e.scalar_tensor_tensor(S.ap[:, dc, :], S.ap[:, dc, :],
                                                                                    EB.ap[:, dc, 127:128], pS.ap,
                                                                                    ALU.mult, ALU.add),
                              reads=[S, EB, pS], writes=[S])
                    fw.op("act", lambda e: e.activation(Sb.ap, S.ap, AF.Copy), reads=[S], writes=[Sb])

                    def tail(on=on, gst=gst, zs=zs, y=y, t=t):
                        fw.op("act", lambda e: e.activation(sl.junk.ap, on.ap, AF.Square, accum_out=gst.ap[:, 0:1]),
                              reads=[on], writes=[sl.junk, gst])
                        rstd_from_ss(gst.ap[:, 0:1], gst.ap[:, 1:2], gst.ap[:, 2:3], 512, [gst])
                        fw.op("dve", lambda e: e.scalar_tensor_tensor(on.ap, on.ap, gst.ap[:, 2:3],
                                                                      gn.ap[:, 512 * h:512 * (h + 1)],
                                                                      ALU.mult, ALU.mult),
                              reads=[on, gst, gn], writes=[on])
                        fw.op("pool", lambda e: e.tensor_tensor(y.ap, on.ap, zs.ap, ALU.mult),
                              reads=[on, zs], writes=[y])
                        fw.dma("sp", Y_d[t * 128:(t + 1) * 128, 512 * h:512 * (h + 1)], y.ap, reads=[y], writes=[Yb])
                    pend.append(tail)
                    yield
                while pend:
                    pend.pop(0)()
                yield

            for pair in range(0, 4, NSL):
                for k in range(NSL):
                    if not (w_prefetched and pair == 0):
                        load_w(pair + k, slots[k])
                gens = [head_gen(pair + k, slots[k]) for k in range(NSL)]
                while gens:
                    for g in list(gens):
                        try:
                            next(g)
                        except StopIteration:
                            gens.remove(g)

        def diff_mixer(li, di, w_prefetched=False):
            w_in = dwin_d[di]
            lam_init = 0.8 - 0.6 * math.exp(-0.3 * li)
            base_mark["diff"] = nc.sbuf_bytes_remaining
            if w_prefetched:
                assert base_mark.get("wnext") == nc.sbuf_bytes_remaining, "w_in prefetch aliasing broken"
            Wh2 = [sb([128, 8, 1024], BF16, "Wd") for _ in range(2)]
            KT = sb([128, 2, LP], BF16, "KT")
            QT = sb([128, 2, LP], BF16, "QT")
            V = sb([128, NT, 258], BF16, "V")
            ZS = sb([128, NT, 256], BF16, "ZS")
            fw.op("pool", lambda e: e.memset(V.ap, 1.0), writes=[V])
            relE = sb([128, 3, 256], F32, "relE")
            Ehi2 = [sb([128, 3, 256], BF16, "Ehi") for _ in range(2)]
            Elo2 = [sb([128, 3, 256], BF16, "Elo") for _ in range(2)]
            cv = sb([128, 8], F32, "cv")
            fw.dma("sp", cv.ap, relb_d[31:32, :].partition_broadcast(128), writes=[cv])
            lv = sb([128, 512], F32, "lv")
            fw.dma("sp", lv.ap, dlam_d[di:di + 1, :].partition_broadcast(128), writes=[lv])
            lt = sb([128, 8], F32, "lt")
            lj = sb([128, 128], F32, "lj")
            for c in range(2):
                fw.op("dve", lambda e, c=c: e.tensor_tensor(lj.ap, lv.ap[:, 256 * c:256 * c + 128],
                                                            lv.ap[:, 256 * c + 128:256 * c + 256], ALU.mult),
                      reads=[lv], writes=[lj])
                fw.op("act", lambda e, c=c: e.activation(lj.ap, lj.ap, AF.Copy, accum_out=lt.ap[:, c:c + 1]),
                      reads=[lj], writes=[lj, lt])
            fw.op("act", lambda e: e.activation(lt.ap[:, 2:4], lt.ap[:, 0:2], AF.Exp), reads=[lt], writes=[lt])
            fw.op("dve", lambda e: e.tensor_tensor(lt.ap[:, 4:5], lt.ap[:, 3:4], lt.ap[:, 2:3], ALU.subtract),
                  reads=[lt], writes=[lt])
            fw.op("dve", lambda e: e.tensor_scalar(lt.ap[:, 5:6], lt.ap[:, 4:5], -lam_init, None, ALU.add),
                  reads=[lt], writes=[lt])
            nlam = lt.ap[:, 5:6]
            gnd = sb([128, 256], F32, "gnd")
            fw.dma("sp", gnd.ap, dgn_d[di:di + 1, :].partition_broadcast(128), writes=[gnd])
            fw.op("dve", lambda e: e.tensor_scalar(gnd.ap, gnd.ap, 0.5 * (1.0 - lam_init), None, ALU.mult),
                  reads=[gnd], writes=[gnd])
            ez2 = [sb([128, 256], F32, "dez") for _ in range(2)]
            PT4 = [sb([128, 2, 256], BF16, "PT") for _ in range(4)]
            cp3 = [sb([128, 2, 258], F32, "cpO") for _ in range(4)]
            y3 = [sb([128, 256], BF16, "dy") for _ in range(4)]
            st3 = [sb([128, 8], F32, "dst") for _ in range(4)]
            junkf = sb([128, 256], F32, "junkf")
            scale = 128.0 ** -0.5
            cnt = {"pt": 0, "tmp": 0, "ep": 0, "ez": 0}

            def load_w(h):
                W = Wh2[h % 2]
                for kc4 in range(0, 8, 4):
                    fw.dma("pool", W.ap[:, kc4:kc4 + 4, :],
                           w_in[h][kc4 * 128:(kc4 + 4) * 128, :].rearrange("(kc p) n -> p kc n", p=128),
                           writes=[W])

            if not w_prefetched:
                load_w(0)
            for h in range(8):
                if h + 1 < 8 and not (w_prefetched and h == 0):
                    load_w(h + 1)
                W = Wh2[h % 2]
                Ehi, Elo = Ehi2[h % 2], Elo2[h % 2]
                fw.dma("sp", relE.ap, relE_d[h], writes=[relE])
                fw.op("dve", lambda e: e.tensor_scalar(relE.ap, relE.ap, 1.0 / scale, None, ALU.mult),
                      reads=[relE], writes=[relE])
                fw.op("dve", lambda e: e.tensor_copy(Ehi.ap, relE.ap), reads=[relE], writes=[Ehi])
                fw.op("dve", lambda e: e.tensor_tensor(Elo.ap, relE.ap, Ehi.ap, ALU.subtract),
                      reads=[relE, Ehi], writes=[Elo])
                for blk in range(4):
                    dst = QT if blk < 2 else KT
                    p = blk % 2
                    for s in range(9):
                        s0 = 512 * s
                        w = min(512, LP - s0)
                        pb = bank()
                        for kc in range(8):
                            fw.op("pe", lambda e, kc=kc: e.matmul(pb.ap[:, 0:w], W.ap[:, kc, blk * 128:(blk + 1) * 128],
                                                                  xnT.ap[:, kc, s0:s0 + w],
                                                                  start=(kc == 0), stop=(kc == 7)),
                                  reads=[W, xnT], writes=[pb], signal=(kc == 7))
                        if (s + blk) % 2 == 0:
                            fw.op("act", lambda e: e.activation(dst.ap[:, p, s0:s0 + w], pb.ap[:, 0:w], AF.Copy),
                                  reads=[pb], writes=[dst])
                        else:
                            fw.op("dve", lambda e: e.tensor_copy(dst.ap[:, p, s0:s0 + w], pb.ap[:, 0:w]),
                                  reads=[pb], writes=[dst])
                for t in range(NT):
                    pb = bank()
                    for kc in range(8):
                        fw.op("pe", lambda e, kc=kc: e.matmul(pb.ap, xnT.ap[:, kc, t * 128:(t + 1) * 128],
                                                              W.ap[:, kc, 512:1024], start=(kc == 0), stop=(kc == 7)),
                              reads=[W, xnT], writes=[pb], signal=(kc == 7))
                    ez = ez2[cnt["ez"] % 2]
                    cnt["ez"] += 1
                    fw.op("act", lambda e: e.activation(V.ap[:, t, 0:256], pb.ap[:, 0:256], AF.Copy),
                          reads=[pb], writes=[V])
                    fw.op("act", lambda e: e.activation(ez.ap, pb.ap[:, 256:512], AF.Tanh, scale=0.5),
                          reads=[pb], writes=[ez])
                    fw.op("dve", lambda e: e.scalar_tensor_tensor(ZS.ap[:, t, :], ez.ap, 1.0, pb.ap[:, 256:512],
                                                                  ALU.add, ALU.mult),
                          reads=[pb, ez], writes=[ZS])
                if h == 7:
                    for half in range(2):
                        fw.dma("pool", Wh2[half].ap,
                               dwo_d[di][half * 1024:(half + 1) * 1024, :].rearrange("(kc p) n -> p kc n", p=128),
                               writes=[Wh2[half]])
                Ob = [[banks[2 * p + sub] for sub in range(2)] for p in range(2)]
                steps = []
                for I in range(17):
                    nk = min(2 * I + 2, NT)
                    for j in range(nk):
                        steps.append((I, j, j == nk - 1))
                sbank = {}

                def emit_qk(si):
                    I, j, _ = steps[si]
                    Wq = 256 if I < 16 else 128
                    q0 = 256 * I
                    pS = bank(4, 8)
                    sbank[si] = pS
                    Sv = pS.ap.rearrange("p (c n) -> p c n", c=2)
                    kind = j - (2 * I - 1)
                    c0 = 128 if kind == 2 else 0
                    for p in range(2):
                        fw.op("pe", lambda e, p=p: e.matmul(Sv[:, p, c0:Wq], KT.ap[:, p, j * 128:(j + 1) * 128],
                                                            QT.ap[:, p, q0 + c0:q0 + Wq], start=True, stop=(kind < 0)),
                              reads=[KT, QT], writes=[pS], signal=(p == 1 and kind < 0))
                        if kind >= 0:
                            fw.op("pe", lambda e, p=p: e.matmul(Sv[:, p, c0:Wq], ident.ap, Ehi.ap[:, kind, c0:Wq],
                                                                start=False, stop=False),
                                  reads=[ident, Ehi], writes=[pS], signal=False)
                            fw.op("pe", lambda e, p=p: e.matmul(Sv[:, p, c0:Wq], ident.ap, Elo.ap[:, kind, c0:Wq],
                                                                start=False, stop=True),
                                  reads=[ident, Elo], writes=[pS], signal=(p == 1))

                def emit_rest(si):
                    I, j, last_step = steps[si]
                    Wq = 256 if I < 16 else 128
                    nsub = Wq // 128
                    pS = sbank.pop(si)
                    Sv = pS.ap.rearrange("p (c n) -> p c n", c=2)
                    PT = PT4[cnt["pt"] % 4]
                    cnt["pt"] += 1
                    kind = j - (2 * I - 1)
                    if kind < 0:
                        fw.op("act", lambda e: e.activation(PT.ap[:, :, 0:Wq], Sv[:, :, 0:Wq], AF.Exp,
                                                            scale=scale, bias=cv.ap[:, h:h + 1]),
                              reads=[pS, cv], writes=[PT])
                    else:
                        c0 = 128 if kind == 2 else 0
                        fw.op("act", lambda e: e.activation(PT.ap[:, :, c0:Wq], Sv[:, :, c0:Wq], AF.Exp, scale=scale),
                              reads=[pS], writes=[PT])
                    mm = []
                    for p in range(2):
                        for sub in range(nsub):
                            if kind == 2 and sub == 0:
                                continue
                            mm.append((p, sub, j == 2 * I + sub))
                    for n_, (p, sub, last) in enumerate(mm):
                        fw.op("pe", lambda e, p=p, sub=sub, last=last: e.matmul(
                            Ob[p][sub].ap[:, 0:257], PT.ap[:, p, sub * 128:(sub + 1) * 128], V.ap[:, j, 0:257],
                            start=(j == 0), stop=last),
                            reads=[PT, V], writes=[Ob[p][sub]], signal=(n_ == len(mm) - 1))
                    for sub in range(nsub):
                        if j != 2 * I + sub:
                            continue
                        qi = 2 * I + sub
                        k3 = cnt["ep"] % 4
                        cnt["ep"] += 1
                        cp, y, dst_ = cp3[k3], y3[k3], st3[k3]
                        O1, O2 = Ob[0][sub], Ob[1][sub]
                        fw.op("dve", lambda e: e.tensor_copy(cp.ap[:, 0, 0:257], O1.ap[:, 0:257]), reads=[O1], writes=[cp])
                        fw.op("dve", lambda e: e.tensor_copy(cp.ap[:, 1, 0:257], O2.ap[:, 0:257]), reads=[O2], writes=[cp])

                        def stage_b(cp=cp, dst_=dst_):
                            fw.op("dve", lambda e: e.reciprocal(dst_.ap[:, 0:2], cp.ap[:, :, 256]), reads=[cp], writes=[dst_])
                            fw.op("dve", lambda e: e.tensor_tensor(dst_.ap[:, 2:3], dst_.ap[:, 1:2], nlam, ALU.mult),
                                  reads=[dst_, lt], writes=[dst_])
                            fw.op("dve", lambda e: e.tensor_scalar(cp.ap[:, 0, 0:256], cp.ap[:, 0, 0:256],
                                                                   dst_.ap[:, 0:1], None, ALU.mult),
                                  reads=[cp, dst_], writes=[cp])
                            fw.op("dve", lambda e: e.scalar_tensor_tensor(cp.ap[:, 1, 0:256], cp.ap[:, 1, 0:256],
                                                                          dst_.ap[:, 2:3], cp.ap[:, 0, 0:256],
                                                                          ALU.mult, ALU.add),
                                  reads=[cp, dst_], writes=[cp])
                            fw.op("dve", lambda e: e.scalar_tensor_tensor(junkf.ap, cp.ap[:, 1, 0:256], 1.0,
                                                                          cp.ap[:, 1, 0:256], ALU.mult, ALU.mult,
                                                                          accum_out=dst_.ap[:, 3:4]),
                                  reads=[cp], writes=[junkf, dst_])
                            rstd_from_ss(dst_.ap[:, 3:4], dst_.ap[:, 4:5], dst_.ap[:, 5:6], 256, [dst_])

                        def stage_c(cp=cp, dst_=dst_, y=y, qi=qi):
                            fw.op("dve", lambda e: e.scalar_tensor_tensor(cp.ap[:, 0, 0:256], cp.ap[:, 1, 0:256],
                                                                          dst_.ap[:, 5:6], gnd.ap, ALU.mult, ALU.mult),
                                  reads=[cp, dst_, gnd], writes=[cp])
                            fw.op("pool", lambda e: e.tensor_tensor(y.ap, cp.ap[:, 0, 0:256], ZS.ap[:, qi, :], ALU.mult),
                                  reads=[cp, ZS], writes=[y])
                            fw.dma("sp", Y_d[qi * 128:(qi + 1) * 128, 256 * h:256 * (h + 1)], y.ap,
                                   reads=[y], writes=[Yb])

                        deferred.append((si + 1, stage_b))
                        deferred.append((si + 4, stage_c))
                    deferred.sort(key=lambda d: d[0])
                    while deferred and deferred[0][0] <= si:
                        deferred.pop(0)[1]()

                deferred = []
                LA = 3
                nq = 0
                for si in range(len(steps)):
                    while nq < min(si + 1 + LA, len(steps)):
                        emit_qk(nq)
                        nq += 1
                    emit_rest(si)
                deferred.sort(key=lambda d: d[0])
                while deferred:
                    deferred.pop(0)[1]()

        last_layer = layers[-1]
        fw.op("pool", lambda e: e.memset(xnT.ap[:, :, LTOK:LP], 0.0), writes=[xnT])
        for n_l, li in enumerate(layers):
            is_gla = (li % 2 == 0)
            order = "real" if is_gla else "pos"
            if n_l == 0:
                with ExitStack() as s1:
                    cur[0] = s1
                    phase_A(li, order, prefetch_gla=(li // 2 if is_gla else None))
                    fw.barrier()
            with ExitStack() as s2:
                cur[0] = s2
                if is_gla:
                    gla_mixer(li, li // 2, w_prefetched=(n_l == 0))
                else:
                    diff_mixer(li, li // 2, w_prefetched=(n_l > 0))
                fw.barrier()
            with ExitStack() as s3:
                cur[0] = s3
                is_last = final and li == last_layer
                phase_D(li, order, gwo_d[li // 2] if is_gla else dwo_d[li // 2], is_last,
                        fuse_next=(li != last_layer), wo_prefetched=(not is_gla),
                        next_diff=((li + 1) // 2 if (is_gla and li != last_layer) else None))
                fw.barrier()
            cur[0] = st
        fw.finish()
        build_program.stats = (fw.n_inst, fw.n_wait, {e.name: e.cnt for e in fw.E.values()})
    return nc


def _t5_bucket_np(n):
    n = np.asarray(n, dtype=np.int64)
    nf = np.maximum(n, 1).astype(np.float32)
    large = 16 + (np.log(nf / np.float32(16)) / np.float32(math.log(128 / 16)) * np.float32(16)).astype(np.int32)
    large = np.minimum(large, 31)
    return np.where(n < 16, n, large)


def _rel_tiles(rel_bias):
    kl = np.arange(128)[:, None]
    ql = np.arange(128)[None, :]
    d0 = ql - kl
    d1 = 128 + ql - kl
    b0 = _t5_bucket_np(np.maximum(d0, 0))
    b1 = _t5_bucket_np(d1)
    out = np.empty((8, 128, 3, 256), np.float32)
    for h in range(8):
        D0 = np.where(d0 >= 0, rel_bias[b0, h], np.float32(NEG)).astype(np.float32)
        D1 = rel_bias[b1, h].astype(np.float32)
        C = np.full((128, 128), rel_bias[31, h], np.float32)
        M = np.full((128, 128), NEG, np.float32)
        out[h, :, 0, :] = np.concatenate([D1, C], 1)
        out[h, :, 1, :] = np.concatenate([D0, D1], 1)
        out[h, :, 2, :] = np.concatenate([M, D0], 1)
    return out


def _consts():
    j = np.arange(128)[:, None]
    i = np.arange(128)[None, :]
    c = np.zeros((128, 5, 128), np.float32)
    c[:, 0, :] = np.eye(128, dtype=np.float32)
    c[:, 1, :] = (j <= i)
    c[:, 2, :] = (j > i)
    c[:, 3, :] = (j <= i) & (j < N_META)
    c[:, 4, :] = (j > i) & (j < N_META)
    return c


def _gla_w_layout(w):
    out = np.empty((2, 4, D_MODEL, 1552), np.float32)
    for h in range(4):
        out[:, h, :, 0:256] = w[:, :, 256 * h:256 * (h + 1)]
        out[:, h, :, 256:512] = w[:, :, 1024 + 256 * h:1024 + 256 * (h + 1)]
        out[:, h, :, 512:1024] = w[:, :, 2048 + 512 * h:2048 + 512 * (h + 1)]
        out[:, h, :, 1024:1536] = w[:, :, 4096 + 512 * h:4096 + 512 * (h + 1)]
        out[:, h, :, 1536:1552] = w[:, :, 6144:6160]
    return out


def _diff_w_layout(w):
    out = np.empty((2, 8, D_MODEL, 1024), np.float32)
    for h in range(8):
        for j in range(4):
            out[:, h, :, 256 * j:256 * (j + 1)] = w[:, :, 2048 * j + 256 * h:2048 * j + 256 * (h + 1)]
    return out


_NC_CACHE = {}


def kernel(x, meta, g_norm, gla_w_in, gla_w_gate_up, gla_b_gate, gla_gnorm, gla_w_out,
           diff_w_in, diff_lam, diff_gnorm, diff_w_out, rel_bias, g_final):
    f = lambda a: np.ascontiguousarray(np.asarray(a, dtype=np.float32))
    x = f(x)
    rel_bias = f(rel_bias)
    shared = {
        "meta": f(meta), "g_norm": f(g_norm), "g_final": f(g_final).reshape(1, D_MODEL),
        "gla_w_in": _gla_w_layout(f(gla_w_in)),
        "gla_wg": np.ascontiguousarray(np.concatenate([f(gla_w_gate_up), f(gla_b_gate)[:, None, :]], axis=1)),
        "gla_gnorm": f(gla_gnorm), "gla_w_out": f(gla_w_out),
        "diff_w_in": _diff_w_layout(f(diff_w_in)), "diff_lam": f(diff_lam).reshape(2, 512), "diff_gnorm": f(diff_gnorm),
        "diff_w_out": f(diff_w_out), "relE": _rel_tiles(rel_bias), "rel_bias": rel_bias, "consts": _consts(),
    }
    if "nc" not in _NC_CACHE:
        _NC_CACHE["nc"] = build_program()
    nc = _NC_CACHE["nc"]
    in_maps = [dict(shared, x=np.ascontiguousarray(x[b])) for b in range(8)]
    res = run_bass_kernel_spmd(nc, in_maps, core_ids=list(range(8)))
    return np.stack([np.asarray(r["out"], dtype=np.float32) for r in res.results], axis=0)
```
